# Optimizing a Trainium2 kernel written in Bass

```python
import math
import jax, jax.numpy as jnp
from jax import lax
import numpy as np

D_MODEL = 1024
BATCH = 8
SEQ = 2048
DEPTH = 2
DEC_BATCH = 128
DEC_SEQ = 1
PAST_LEN = 16384
PAGE_SIZE = 128

RET_HEADS = 4
RET_DK = 128
RET_DV = 256
ROPE_BASE = 10000.0
HG_HEADS = 8
HG_EXPAND = 128
HG_DV = 128
SSM_HEADS = 16
SSM_HEADDIM = 64
SSM_GROUPS = 4
SSM_STATE = 128
SSM_CONV = 4
SSM_INNER = SSM_HEADS * SSM_HEADDIM
SSM_CONV_CH = SSM_INNER + 2 * SSM_GROUPS * SSM_STATE
RET_Q = RET_HEADS * RET_DK
RET_V = RET_HEADS * RET_DV
HG_K = HG_HEADS * HG_EXPAND
HG_V = HG_HEADS * HG_DV
BRANCH_W = 1024
N_BRANCH = 3
D_FF = 4 * D_MODEL
CHUNK = 64
EPS = 1e-6
SPLITS = (RET_Q, RET_Q, RET_V, RET_V, HG_K, HG_K, HG_V, HG_V, SSM_INNER, SSM_CONV_CH, SSM_HEADS, N_BRANCH * D_MODEL)
D_IN_PROJ = sum(SPLITS)

kernel_name = "hybrid_retention_hgrn2_ssd_step"


def rmsnorm(x, gain):
    xf = x.astype(jnp.float32)
    y = xf * lax.rsqrt(jnp.mean(xf * xf, axis=-1, keepdims=True) + EPS)
    return (y * gain.astype(jnp.float32)).astype(x.dtype)


def head_layernorm(o, gain):
    mu = jnp.mean(o, axis=-1, keepdims=True)
    oc = o - mu
    return oc * lax.rsqrt(jnp.mean(oc * oc, axis=-1, keepdims=True) + EPS) * gain.astype(jnp.float32)


def head_rmsnorm(o, gain):
    return o * lax.rsqrt(jnp.mean(o * o, axis=-1, keepdims=True) + EPS) * gain.astype(jnp.float32)


def rope(t, pos):
    half = t.shape[-1] // 2
    inv = ROPE_BASE ** (-jnp.arange(half, dtype=jnp.float32) / half)
    ang = pos.astype(jnp.float32)[:, None] * inv[None, :]
    cos = jnp.cos(ang)[None, :, None, :]
    sin = jnp.sin(ang)[None, :, None, :]
    tf = t.astype(jnp.float32)
    t1, t2 = tf[..., :half], tf[..., half:]
    return jnp.concatenate([t1 * cos - t2 * sin, t1 * sin + t2 * cos], axis=-1)


def chunk_len(L):
    return CHUNK if L % CHUNK == 0 else math.gcd(L, CHUNK)


def chunked_gla(q, k, v, g, s0):
    B, L, H, K = q.shape
    V = v.shape[-1]
    c = chunk_len(L)
    n = L // c
    per_channel = g.shape[-1] != 1

    def to_chunks(a):
        return a.astype(jnp.float32).reshape(B, n, c, H, a.shape[-1]).transpose(1, 0, 3, 2, 4)

    qc, kc, vc, gc = to_chunks(q), to_chunks(k), to_chunks(v), to_chunks(g)
    causal = jnp.tril(jnp.ones((c, c), dtype=bool))

    def step(S, inp):
        qi, ki, vi, gi = inp
        G = jnp.cumsum(gi, axis=2)
        Gl = G[:, :, -1:, :]
        if per_channel:
            diff = G[:, :, :, None, :] - G[:, :, None, :, :]
            m = causal[:, :, None]
            dec = jnp.where(m, jnp.exp(jnp.where(m, diff, 0.0)), 0.0)
            att = jnp.einsum('bhik,bhjk,bhijk->bhij', qi, ki, dec)
        else:
            Gs = G[..., 0]
            diff = Gs[:, :, :, None] - Gs[:, :, None, :]
            dec = jnp.where(causal, jnp.exp(jnp.where(causal, diff, 0.0)), 0.0)
            att = jnp.einsum('bhik,bhjk->bhij', qi, ki) * dec
        o = jnp.einsum('bhij,bhjv->bhiv', att, vi) + jnp.einsum('bhik,bhkv->bhiv', qi * jnp.exp(G), S)
        S_new = jnp.exp(Gl[:, :, 0, :])[..., None] * S + jnp.einsum('bhjk,bhjv->bhkv', ki * jnp.exp(Gl - G), vi)
        return S_new, o

    S_fin, oc = lax.scan(step, s0.astype(jnp.float32), (qc, kc, vc, gc))
    o = oc.transpose(1, 0, 3, 2, 4).reshape(B, L, H, V)
    return o, S_fin


def causal_conv(u, buf, w, b):
    full = jnp.concatenate([buf.astype(u.dtype), u], axis=1)
    out = lax.conv_general_dilated(full, w[:, None, :].astype(u.dtype), window_strides=(1,), padding='VALID',
                                   dimension_numbers=('NWC', 'WIO', 'NWC'), feature_group_count=u.shape[-1])
    return out + b.astype(u.dtype), full[:, -(SSM_CONV - 1):]


def mixer(xn, pos, lb, s_ret, s_hg, s_ssm, s_conv, w_in, ret_norm, hg_norm, conv_w, conv_b,
          dt_bias, a_log, d_skip, ssm_norm, w_branch, w_out):
    B, L, _ = xn.shape
    f32 = jnp.float32
    proj = xn @ w_in
    rq, rk, rv, rg, hq, hf, hi, hgt, z, xbc, dt, gates = jnp.split(proj, np.cumsum(SPLITS)[:-1].tolist(), axis=-1)

    q_r = rope(rq.reshape(B, L, RET_HEADS, RET_DK), pos)
    k_r = rope(rk.reshape(B, L, RET_HEADS, RET_DK), pos) * (RET_DK ** -0.5)
    v_r = rv.reshape(B, L, RET_HEADS, RET_DV)
    log_gamma = jnp.log1p(-jnp.exp2(-5.0 - jnp.arange(RET_HEADS, dtype=f32)))
    g_r = jnp.broadcast_to(log_gamma[:, None], (B, L, RET_HEADS, 1))
    o_r, s_ret_new = chunked_gla(q_r, k_r, v_r, g_r, s_ret)
    o_r = head_layernorm(o_r, ret_norm.reshape(RET_HEADS, RET_DV)).reshape(B, L, RET_V)
    y_r = (jax.nn.silu(rg.astype(f32)) * o_r).astype(xn.dtype)

    q_h = jax.nn.silu(hq.reshape(B, L, HG_HEADS, HG_EXPAND).astype(f32))
    lbh = lb.reshape(HG_HEADS, HG_EXPAND)
    f_h = lbh + (1.0 - lbh) * jax.nn.sigmoid(hf.reshape(B, L, HG_HEADS, HG_EXPAND).astype(f32))
    f_h = jnp.clip(f_h, 1e-6, 1.0)
    g_h = jnp.log(f_h)
    k_h = 1.0 - f_h
    o_h, s_hg_new = chunked_gla(q_h, k_h, hi.reshape(B, L, HG_HEADS, HG_DV), g_h, s_hg)
    o_h = head_rmsnorm(o_h, hg_norm.reshape(HG_HEADS, HG_DV)).reshape(B, L, HG_V)
    y_h = (o_h * jax.nn.sigmoid(hgt.astype(f32))).astype(xn.dtype)

    xbc_c, s_conv_new = causal_conv(xbc, s_conv, conv_w, conv_b)
    xbc_c = jax.nn.silu(xbc_c.astype(f32))
    xs, Bm, Cm = jnp.split(xbc_c, [SSM_INNER, SSM_INNER + SSM_GROUPS * SSM_STATE], axis=-1)
    xs = xs.reshape(B, L, SSM_HEADS, SSM_HEADDIM)
    rep = SSM_HEADS // SSM_GROUPS
    Bm = jnp.repeat(Bm.reshape(B, L, SSM_GROUPS, SSM_STATE), rep, axis=2)
    Cm = jnp.repeat(Cm.reshape(B, L, SSM_GROUPS, SSM_STATE), rep, axis=2)
    dtv = jax.nn.softplus(dt.astype(f32) + dt_bias.astype(f32))
    A = -jnp.exp(a_log.astype(f32))
    o_m, s_ssm_new = chunked_gla(Cm, Bm * dtv[..., None], xs, (dtv * A)[..., None], s_ssm)
    o_m = (o_m + d_skip.astype(f32)[:, None] * xs).reshape(B, L, SSM_INNER)
    y_m = rmsnorm(o_m * jax.nn.silu(z.astype(f32)), ssm_norm).astype(xn.dtype)

    ys = jnp.stack([y_r, y_h, y_m], axis=2)
    branch = jnp.einsum('blnw,nwd->blnd', ys, w_branch)
    gate = jax.nn.sigmoid(gates.reshape(B, L, N_BRANCH, D_MODEL).astype(f32))
    merged = jnp.sum(gate * branch.astype(f32), axis=2).astype(xn.dtype)
    return merged @ w_out, s_ret_new, s_hg_new, s_ssm_new, s_conv_new


def block(x, pos, lb, s_ret, s_hg, s_ssm, s_conv, w_in, ret_norm, hg_norm, conv_w, conv_b, dt_bias, a_log,
          d_skip, ssm_norm, w_branch, w_out, n_mix_pre, n_mix_post, n_ffn_pre, n_ffn_post, w_up, w_down):
    m, s_ret, s_hg, s_ssm, s_conv = mixer(rmsnorm(x, n_mix_pre), pos, lb, s_ret, s_hg, s_ssm, s_conv, w_in,
                                          ret_norm, hg_norm, conv_w, conv_b, dt_bias, a_log, d_skip,
                                          ssm_norm, w_branch, w_out)
    x = x + rmsnorm(m, n_mix_post)
    f = jnp.square(jax.nn.relu(rmsnorm(x, n_ffn_pre) @ w_up)) @ w_down
    x = x + rmsnorm(f, n_ffn_post)
    return x, s_ret, s_hg, s_ssm, s_conv


def setup_inputs(seed: int = 0) -> dict:
    key = jax.random.key(seed)
    ks = jax.random.split(key, 32)

    def nrm(k, shape, s):
        return jax.random.normal(k, shape, jnp.float32) * s

    dt0 = jnp.exp(jax.random.uniform(ks[12], (DEPTH, SSM_HEADS), jnp.float32, math.log(1e-3), math.log(1e-1)))
    return {
        "x_prompt": nrm(ks[0], (BATCH, SEQ, D_MODEL), 1.0),
        "x_sample": nrm(ks[1], (DEC_BATCH, DEC_SEQ, D_MODEL), 1.0),
        "state_ret": nrm(ks[2], (DEPTH, DEC_BATCH, RET_HEADS, RET_DK, RET_DV), 0.5),
        "state_hgrn": nrm(ks[3], (DEPTH, DEC_BATCH, HG_HEADS, HG_EXPAND, HG_DV), 0.5),
        "state_ssm": nrm(ks[4], (DEPTH, DEC_BATCH, SSM_HEADS, SSM_STATE, SSM_HEADDIM), 0.5),
        "state_conv": nrm(ks[5], (DEPTH, DEC_BATCH, SSM_CONV - 1, SSM_CONV_CH), 1.0),
        "w_in": nrm(ks[6], (DEPTH, D_MODEL, D_IN_PROJ), D_MODEL ** -0.5),
        "ret_norm": 1.0 + nrm(ks[7], (DEPTH, RET_V), 0.02),
        "hg_norm": 1.0 + nrm(ks[8], (DEPTH, HG_V), 0.02),
        "hg_lb_logits": nrm(ks[9], (DEPTH, HG_K), 1.0),
        "conv_w": nrm(ks[10], (DEPTH, SSM_CONV, SSM_CONV_CH), SSM_CONV ** -0.5),
        "conv_b": nrm(ks[11], (DEPTH, SSM_CONV_CH), 0.02),
        "dt_bias": dt0 + jnp.log(-jnp.expm1(-dt0)),
        "a_log": jnp.log(jax.random.uniform(ks[13], (DEPTH, SSM_HEADS), jnp.float32, 1.0, 16.0)),
        "d_skip": 1.0 + nrm(ks[14], (DEPTH, SSM_HEADS), 0.1),
        "ssm_norm": 1.0 + nrm(ks[15], (DEPTH, SSM_INNER), 0.02),
        "w_branch": nrm(ks[16], (DEPTH, N_BRANCH, BRANCH_W, D_MODEL), BRANCH_W ** -0.5),
        "w_out": nrm(ks[17], (DEPTH, D_MODEL, D_MODEL), D_MODEL ** -0.5),
        "norm_mix_pre": 1.0 + nrm(ks[18], (DEPTH, D_MODEL), 0.02),
        "norm_mix_post": 1.0 + nrm(ks[19], (DEPTH, D_MODEL), 0.02),
        "norm_ffn_pre": 1.0 + nrm(ks[20], (DEPTH, D_MODEL), 0.02),
        "norm_ffn_post": 1.0 + nrm(ks[21], (DEPTH, D_MODEL), 0.02),
        "w_up": nrm(ks[22], (DEPTH, D_MODEL, D_FF), D_MODEL ** -0.5),
        "w_down": nrm(ks[23], (DEPTH, D_FF, D_MODEL), D_FF ** -0.5),
    }


def reference(x_prompt, x_sample, state_ret, state_hgrn, state_ssm, state_conv, w_in, ret_norm, hg_norm,
              hg_lb_logits, conv_w, conv_b, dt_bias, a_log, d_skip, ssm_norm, w_branch, w_out, norm_mix_pre,
              norm_mix_post, norm_ffn_pre, norm_ffn_post, w_up, w_down):
    f32 = jnp.float32
    Bp, Lp, _ = x_prompt.shape
    Bs, Ls, _ = x_sample.shape
    pos_p = jnp.arange(Lp)
    pos_s = PAST_LEN + jnp.arange(Ls)
    lb_w = jax.nn.softmax(hg_lb_logits.astype(f32), axis=0)
    lb_all = jnp.cumsum(lb_w, axis=0) - lb_w[0]

    sp = (jnp.zeros((Bp, RET_HEADS, RET_DK, RET_DV), f32), jnp.zeros((Bp, HG_HEADS, HG_EXPAND, HG_DV), f32),
          jnp.zeros((Bp, SSM_HEADS, SSM_STATE, SSM_HEADDIM), f32), jnp.zeros((Bp, SSM_CONV - 1, SSM_CONV_CH), x_prompt.dtype))
    hp, hs = x_prompt, x_sample
    p_ret, p_hg, p_ssm, p_conv = [], [], [], []
    s_ret, s_hg, s_ssm, s_conv = [], [], [], []
    for l in range(DEPTH):
        lp = (w_in[l], ret_norm[l], hg_norm[l], conv_w[l], conv_b[l], dt_bias[l], a_log[l], d_skip[l], ssm_norm[l],
              w_branch[l], w_out[l], norm_mix_pre[l], norm_mix_post[l], norm_ffn_pre[l], norm_ffn_post[l], w_up[l], w_down[l])
        hp, a, b, c, d = block(hp, pos_p, lb_all[l], sp[0], sp[1], sp[2], sp[3], *lp)
        p_ret.append(a); p_hg.append(b); p_ssm.append(c); p_conv.append(d)
        hs, a, b, c, d = block(hs, pos_s, lb_all[l], state_ret[l], state_hgrn[l], state_ssm[l], state_conv[l], *lp)
        s_ret.append(a); s_hg.append(b); s_ssm.append(c); s_conv.append(d)
    return (hp, hs, jnp.stack(p_ret), jnp.stack(p_hg), jnp.stack(p_ssm), jnp.stack(p_conv),
            jnp.stack(s_ret), jnp.stack(s_hg), jnp.stack(s_ssm), jnp.stack(s_conv))
```

```python
import contextlib
import math
import numpy as np
import concourse.bass as bass
import concourse.mybir as mybir
from concourse.bass_utils import run_bass_kernel_spmd

F32 = mybir.dt.float32
BF16 = mybir.dt.bfloat16
AF = mybir.ActivationFunctionType
ALU = mybir.AluOpType

D = 1024
SEQ = 2048
TB = 512
NBLK = SEQ // TB
C = 64
NCH = TB // C
NS = 16
DEPTH = 2
DIN = 13328
EPS = 1e-6
PAST = 16384
RUN_SAMPLES = True
HOIST = True
DBG = {"blocks": NBLK, "layers": DEPTH, "stages": ("ret", "hg", "ssd", "ffn")}


class T:
    def __init__(self, ap, parent=None, excl=False):
        self.ap = ap
        self._p = parent if parent is not None else self
        self._w = None
        self._r = []
        self.excl = excl or (parent is not None and parent.excl)

    @property
    def w(self):
        return self._p._w

    @w.setter
    def w(self, v):
        self._p._w = v

    @property
    def r(self):
        return self._p._r

    @r.setter
    def r(self, v):
        self._p._r = v


class TG:
    def __init__(self, ap, n=3):
        self.ap = ap
        self.members = [T(ap) for _ in range(n)]


def _flat(lst):
    out = []
    for t in lst:
        if hasattr(t, "members"):
            out.extend(t.members)
        else:
            out.append(t)
    return out


class Op:
    __slots__ = ("eng", "fn", "deps", "has_dep", "event", "is_dma", "idx")

    def __init__(self, eng, fn, is_dma=False):
        self.eng = eng
        self.fn = fn
        self.deps = []
        self.has_dep = False
        self.event = None
        self.is_dma = is_dma


ENGS = ("pe", "act", "dve", "pool", "sp")
RING = 8


class Prog:
    def __init__(self, nc):
        self.nc = nc
        self.ops = {e: [] for e in ENGS}
        self.n = 0

    def op(self, eng, fn, reads=(), writes=(), is_dma=False, hoist=False):
        o = Op(eng, fn, is_dma)
        self.n += 1
        o.idx = self.n
        deps = {}
        reads = _flat(reads)
        writes = _flat(writes)
        ex = [t for t in reads if t.excl]
        if ex:
            reads = [t for t in reads if not t.excl]
            writes = list(writes) + ex
        for t in reads:
            if t.w is not None:
                deps[id(t.w)] = t.w
        for t in writes:
            if t.w is not None:
                deps[id(t.w)] = t.w
            for rr in t.r:
                deps[id(rr)] = rr
        for d in deps.values():
            if d is o:
                continue
            if eng == "pe" and d.eng == "pe" and not d.is_dma:
                continue
            o.deps.append(d)
            d.has_dep = True
        for t in reads:
            t.r.append(o)
        for t in writes:
            t.w = o
            t.r = []
        lst = self.ops[eng]
        if hoist:
            md = max([d.idx for d in o.deps], default=0)
            p = len(lst)
            while p > 0 and lst[p - 1].idx > md and not lst[p - 1].is_dma:
                p -= 1
            lst.insert(p, o)
        else:
            lst.append(o)
        return o

    def dma(self, eng, out_ap, in_ap, reads=(), writes=(), hoist=False):
        return self.op(eng, lambda e: e.dma_start(out=out_ap, in_=in_ap), reads, writes, is_dma=True, hoist=hoist)

    def emit(self):
        nc = self.nc
        with contextlib.ExitStack() as st:
            eng_sem = {e: st.enter_context(nc.semaphore("s_" + e)) for e in ENGS}
            rings = {}
            tails = {}
            for e in ENGS:
                ndma = sum(1 for o in self.ops[e] if o.is_dma)
                rings[e] = [st.enter_context(nc.semaphore("r_%s%d" % (e, i))) for i in range(min(RING, ndma))]
            for e in ENGS:
                cnt = 0
                k = 0
                dmas = []
                for o in self.ops[e]:
                    if o.is_dma:
                        R = len(rings[e])
                        o.event = (rings[e][k % R], 16 * (k // R + 1))
                        if k >= R:
                            o.deps.append(dmas[k - R])
                        dmas.append(o)
                        k += 1
                    elif o.has_dep:
                        cnt += 1
                        o.event = (eng_sem[e], cnt)
                tails[e] = dmas[-RING:]
            block = st.enter_context(nc.Block())
            handles = {"pe": block.tensor, "act": block.scalar, "dve": block.vector, "pool": block.gpsimd, "sp": block.sync}

            def make(e):
                def body(eh):
                    known = {}
                    for o in self.ops[e]:
                        for d in o.deps:
                            sem, val = d.event
                            if known.get(id(sem), 0) < val:
                                eh.wait_ge(sem, val)
                                known[id(sem)] = val
                        ins = o.fn(eh)
                        if o.is_dma:
                            ins.then_inc(o.event[0], 16)
                        elif o.has_dep:
                            ins.then_inc(o.event[0], 1)
                    for d in tails[e]:
                        sem, val = d.event
                        if known.get(id(sem), 0) < val:
                            eh.wait_ge(sem, val)
                            known[id(sem)] = val
                return body

            for e in ENGS:
                handles[e](make(e))


def _consts():
    f = np.float32
    c = {}
    half = 64
    inv = (np.float32(10000.0) ** (-np.arange(half, dtype=f) / np.float32(half))).astype(f)
    pos = np.concatenate([np.arange(SEQ, dtype=f), np.full((NS,), PAST, dtype=f)])
    ang = (pos[:, None] * inv[None, :]).astype(f)
    cs = np.cos(ang).astype(f).T
    sn = np.sin(ang).astype(f).T
    c["cos"] = np.concatenate([cs, cs], 0)
    c["sin"] = np.concatenate([-sn, sn], 0)
    gam = [1.0 - 2.0 ** (-5.0 - h) for h in range(4)]
    j = np.arange(C)[:, None]
    i = np.arange(C)[None, :]
    causal = (j <= i)
    mr = np.zeros((128, 4, C), f)
    gq = np.zeros((128, 4, C), f)
    kds = np.zeros((128, 4), f)
    for h in range(4):
        lg = math.log1p(-2.0 ** (-5.0 - h))
        mr[:C, h, :] = np.where(causal, np.exp(lg * (i - j)), 0.0) * (128 ** -0.5)
        gq[:, h, :] = np.exp(lg * (np.arange(C) + 1))[None, :]
        kds[:C, h] = np.exp(lg * (C - 1 - np.arange(C))) * (128 ** -0.5)
    c["maskret"] = mr.reshape(128, 4 * C)
    c["gq"] = gq.reshape(128, 4 * C)
    c["kds"] = kds
    c["gamC"] = np.array([[math.exp(math.log1p(-2.0 ** (-5.0 - h)) * C) for h in range(4)]] * 128, f)
    c["gam1"] = np.array([[math.exp(math.log1p(-2.0 ** (-5.0 - h))) for h in range(4)]] * 128, f)
    m01 = np.zeros((128, C), f)
    m01[:C] = causal
    c["mask01"] = m01
    ng = np.zeros((128, C), f)
    ng[:C] = np.where(causal, 0.0, -30000.0)
    c["negmask"] = ng
    c["ones"] = np.ones((128, 128), f)
    c["ident"] = np.eye(128, dtype=f)
    rs = np.ones((128, TB), f)
    rs[:, ::C] = 0.0
    c["reset"] = rs
    e16 = np.zeros((128, 16, 16), f)
    for b in range(16):
        e16[:, b, b] = 1.0
    c["eye16rep"] = e16.reshape(128, 256)
    e16p = np.zeros((128, 16), f)
    e16p[:16] = np.eye(16)
    c["eye16p"] = e16p
    rope = np.ascontiguousarray(np.concatenate([c.pop("cos"), c.pop("sin")], 1))
    c["reset"] = c.pop("reset")
    offs = {}
    o = 0
    arrs = []
    for k, v in c.items():
        offs[k] = (o, v.shape[1])
        o += v.shape[1]
        arrs.append(v)
    return np.ascontiguousarray(np.concatenate(arrs, 1)), offs, rope


def _fm(v):
    return np.ascontiguousarray(v.reshape(-1, 128).T)


VEC_NAMES = [("n_mix_pre", 8), ("n_mix_post", 8), ("n_ffn_pre", 8), ("n_ffn_post", 8), ("ret_norm", 8), ("hg_norm", 8),
             ("ssm_norm", 8), ("lb0", 8), ("lb1", 8), ("conv_w", 64), ("conv_b", 16), ("dt_bias", 16), ("a_log", 16), ("d_skip", 16)]
VOFF = {}
_o = 0
for _n, _w in VEC_NAMES:
    VOFF[_n] = _o
    _o += _w
NVL = _o


def build(coffs, NCF):
    nc = bass.Bass("TRN2", target_bir_lowering=False)
    dt_in = lambda name, shape: nc.dram_tensor(name, shape, F32, kind="ExternalInput").ap()
    dt_out = lambda name, shape: nc.dram_tensor(name, shape, F32, kind="ExternalOutput").ap()
    xT = dt_in("xT", [D, SEQ])
    xsT = dt_in("xsT", [D, NS])
    w_in = dt_in("w_in", [DEPTH, D, DIN])
    w_branch = dt_in("w_branch", [DEPTH, 3, D, D])
    w_out = dt_in("w_out", [DEPTH, D, D])
    w_up = dt_in("w_up", [DEPTH, D, 4 * D])
    w_down = dt_in("w_down", [DEPTH, 4 * D, D])
    vecs = dt_in("vecs", [128, DEPTH * NVL])
    cst = dt_in("cst", [128, NCF])
    rope = dt_in("rope", [128, 2 * (SEQ + NS)])
    sret = dt_in("sret", [DEPTH, NS, 4, 128, 256])
    shg = dt_in("shg", [DEPTH, NS, 8, 128, 128])
    sssm = dt_in("sssm", [DEPTH, NS, 16, 128, 64])
    sconvP = dt_in("sconvP", [DEPTH, 128, 16 * NS * 3])
    yT = dt_out("yT", [D, SEQ])
    ysT = dt_out("ysT", [D, NS])
    pret = dt_out("pret", [DEPTH, 4, 128, 256])
    phg = dt_out("phg", [DEPTH, 8, 128, 128])
    pssm = dt_out("pssm", [DEPTH, 16, 128, 64])
    pconvP = dt_out("pconvP", [DEPTH, 128, 48])
    oret = dt_out("oret", [DEPTH, NS, 4, 128, 256])
    ohg = dt_out("ohg", [DEPTH, NS, 8, 128, 128])
    ossm = dt_out("ossm", [DEPTH, NS, 16, 128, 64])
    oconvP = dt_out("oconvP", [DEPTH, 128, 16 * NS * 3])

    P = Prog(nc)
    st = contextlib.ExitStack()
    _cnt = [0]

    def sb(shape, dt):
        _cnt[0] += 1
        return st.enter_context(nc.sbuf_tensor("t%d" % _cnt[0], shape, dt))[:]

    def tl(shape, dt):
        return T(sb(shape, dt))

    def ACT(out, in_, func, r, w, scale=1.0, bias=0.0):
        P.op("act", lambda e: e.activation(out=out, in_=in_, func=func, scale=scale, bias=bias), r, w)

    def TT(eng, out, a, b, op, r, w):
        P.op(eng, lambda e: e.tensor_tensor(out=out, in0=a, in1=b, op=op), r, w)

    def TS(eng, out, a, s1, s2, op0, op1, r, w):
        P.op(eng, lambda e: e.tensor_scalar(out=out, in0=a, scalar1=s1, scalar2=s2, op0=op0, op1=op1), r, w)

    def STT(out, a, s, b, op0, op1, r, w):
        P.op("dve", lambda e: e.scalar_tensor_tensor(out=out, in0=a, scalar=s, in1=b, op0=op0, op1=op1), r, w)

    def CP(eng, out, in_, r, w):
        if eng == "act":
            P.op("act", lambda e: e.copy(out=out, in_=in_), r, w)
        else:
            P.op(eng, lambda e: e.tensor_copy(out=out, in_=in_), r, w)

    def RSQ(out, in_, scale, r, w):
        ACT(out, in_, AF.Ln, r, w, scale=scale, bias=EPS)
        ACT(out, out, AF.Exp, w, w, scale=-0.5)

    def SILU(dst, p, N):
        tmp = nxt(TMPF, "tmp")
        ACT(tmp.ap[:, 0:N], p.ap[:, 0:N], AF.Sigmoid, [p], [tmp])
        TT("dve", dst.ap[:, 0:N], p.ap[:, 0:N], tmp.ap[:, 0:N], ALU.mult, [p, tmp], [dst])

    def MM(out, lhsT, rhs, r, w, start=True, stop=True):
        P.op("pe", lambda e: e.matmul(out, lhsT=lhsT, rhs=rhs, start=start, stop=stop), r, w)

    def TR(out, in_, ident, r, w):
        P.op("pe", lambda e: e.transpose(out, in_, ident), r, w)

    def MS(eng, ap, val, w):
        P.op(eng, lambda e: e.memset(ap, val), (), w)

    cstT = tl([128, NCF - TB], F32)
    P.dma("sp", cstT.ap, cst[:, 0:NCF - TB], writes=[cstT])
    vecT = tl([128, DEPTH * NVL], F32)
    P.dma("sp", vecT.ap, vecs, writes=[vecT])

    def cf(name, a=0, b=None):
        o, wd = coffs[name]
        if b is None:
            b = wd
        return cstT.ap[:, o + a:o + b]

    def vc(l, name, a=0, b=1):
        o = l * NVL + VOFF[name]
        return vecT.ap[:, o + a:o + b]

    cbT = tl([128, 128 * 2 + TB], BF16)
    ones_b = cbT.ap[:, 0:128]
    ident_b = cbT.ap[:, 128:256]
    CP("act", ones_b, cf("ones"), [cstT], [cbT])
    CP("act", ident_b, cf("ident"), [cstT], [cbT])
    reset_b = cbT.ap[:, 256:256 + TB]
    P.dma("pool", reset_b, cst[:, NCF - TB:NCF], writes=[cbT])
    ones_f = cf("ones")
    ident_f = cf("ident")
    lbT = tl([128, 2, 8], F32)
    omlT = tl([128, 2, 8], F32)
    MS("dve", lbT.ap[:, 0, :], 0.0, [lbT])
    dl = tl([128, 8], F32)
    TT("dve", dl.ap, vc(0, "lb1", 0, 8), vc(0, "lb0", 0, 8), ALU.subtract, [vecT], [dl])
    ACT(lbT.ap[:, 1, :], dl.ap, AF.Sigmoid, [dl], [lbT])
    TS("dve", omlT.ap, lbT.ap, -1.0, 1.0, ALU.mult, ALU.add, [lbT], [omlT])
    AnT = tl([128, 2, 16], F32)
    for l in range(DEPTH):
        ACT(AnT.ap[:, l, :], vc(l, "a_log", 0, 16), AF.Exp, [vecT], [AnT])
    TS("dve", AnT.ap, AnT.ap, -1.0, None, ALU.mult, ALU.bypass, [AnT], [AnT])

    psF = st.enter_context(nc.psum_tensor("psF", [128, 7, 512], F32))
    psB = st.enter_context(nc.psum_tensor("psB", [128, 1024], BF16))
    PD = [T(psF[:, i, :], excl=True) for i in range(2)]
    _pa = T(psF[:, 2, :], excl=True)
    PA = [T(psF[:, 2, 0:256], _pa), T(psF[:, 2, 256:512], _pa)]
    PSr = T(psF[:, 3:5, :], excl=True)
    POr = T(psF[:, 5:7, :], excl=True)
    PO = [T(psF[:, 5, 0:256], POr), T(psF[:, 6, 0:256], POr)]
    PS = [T(psF[:, 3, 0:256], PSr), T(psF[:, 4, 0:256], PSr)]
    _pt = T(psB[:, :], excl=True)
    PT = [T(psB[:, i * 128:(i + 1) * 128], _pt) for i in range(8)]
    PTr = _pt
    PAr = _pa

    def psv(base, col, width, rows=128):
        return psF[0:rows, base + col // 512, (col % 512):(col % 512) + width]

    rr = {"pd": 0, "pt": 0, "pa": 0, "po": 0, "ps": 0, "w": 0, "tmp": 0}

    def nxt(lst, key):
        t = lst[rr[key] % len(lst)]
        rr[key] += 1
        return t

    X = [tl([128, TB], F32) for _ in range(8)]
    XN = [tl([128, TB], BF16) for _ in range(8)]
    MG = [tl([128, TB], F32) for _ in range(8)]
    YB = [tl([128, TB], BF16) for _ in range(8)]
    HB = [tl([128, TB], BF16) for _ in range(8)]
    SQ = HB
    WB = [TG(sb([128, 2048], BF16)) for _ in range(5)]
    TMPF = [tl([128, TB], F32) for _ in range(2)]
    rstdT = tl([128, TB], F32)
    rstdM = tl([128, TB], F32)
    cosT = tl([128, TB], F32)
    sinT = tl([128, TB], F32)
    Sret = [[tl([128, 256], F32) for _ in range(4)] for _ in range(DEPTH)]
    Shg = [[tl([128, 128], F32) for _ in range(8)] for _ in range(DEPTH)]
    Sssm = [[tl([128, 256], F32) for _ in range(4)] for _ in range(DEPTH)]
    SretB = [[tl([128, 256], BF16) for _ in range(4)] for _ in range(DEPTH)]
    ShgB = [[tl([128, 128], BF16) for _ in range(8)] for _ in range(DEPTH)]
    SssmB = [[tl([128, 256], BF16) for _ in range(4)] for _ in range(DEPTH)]
    HALO = [[tl([128, 3], F32) for _ in range(16)] for _ in range(DEPTH)]
    for l in range(DEPTH):
        for lst in (Sret[l], Shg[l], Sssm[l], SretB[l], ShgB[l], SssmB[l], HALO[l]):
            for t in lst:
                MS("dve", t.ap, 0.0, [t])

    def LW(src2d, KT, ncols, parts=None):
        wt = nxt(WB, "w")
        view = wt.ap[:, 0:KT * ncols].rearrange("p (kt n) -> p kt n", kt=KT)
        if parts is not None and DBG.get("noparts"):
            P.dma("pool", view[:, :, 0:64], parts[1][2].rearrange("(kt p) n -> p kt n", p=128), writes=[wt])
        elif parts is None:
            P.dma("pool", view, src2d.rearrange("(kt p) n -> p kt n", p=128), writes=[wt], hoist=HOIST)
        else:
            for pi, (c0, n, s) in enumerate(parts):
                P.dma("pool", view[:, :, c0:c0 + n], s.rearrange("(kt p) n -> p kt n", p=128), writes=[wt.members[pi]], hoist=HOIST)
        return wt, view

    PD4 = PD + [T(psF[:, 3, :], PSr), T(psF[:, 5, :], POr)]
    dense_mode = [False]

    def proj(wt, wv, m0, rhs_tiles, N, KT=8, mw=128):
        pd = nxt(PD4, "pd") if dense_mode[0] else nxt(PD, "pd")
        for kt in range(KT):
            MM(pd.ap[0:mw, 0:N], wv[:, kt, m0:m0 + mw], rhs_tiles[kt].ap[:, 0:N], [wt, rhs_tiles[kt]], [pd], start=(kt == 0), stop=(kt == KT - 1))
        return pd

    def rmsnorm_to_bf16(src, gain_name, l, N, dst):
        for t in range(8):
            ACT(SQ[t].ap[:, 0:N], src[t].ap[:, 0:N], AF.Square, [src[t]], [SQ[t]])
        pd = nxt(PD, "pd")
        for t in range(8):
            MM(pd.ap[:, 0:N], ones_b, SQ[t].ap[:, 0:N], [cbT, SQ[t]], [pd], start=(t == 0), stop=(t == 7))
        RSQ(rstdT.ap[:, 0:N], pd.ap[:, 0:N], 1.0 / D, [pd], [rstdT])
        for t in range(8):
            STT(dst[t].ap[:, 0:N], src[t].ap[:, 0:N], vc(l, gain_name, t, t + 1), rstdT.ap[:, 0:N], ALU.mult, ALU.mult, [src[t], vecT, rstdT], [dst[t]])

    def resid_add(l, gain_name, N):
        for t in range(8):
            ACT(SQ[t].ap[:, 0:N], MG[t].ap[:, 0:N], AF.Square, [MG[t]], [SQ[t]])
        pd = nxt(PD, "pd")
        for t in range(8):
            MM(pd.ap[:, 0:N], ones_b, SQ[t].ap[:, 0:N], [cbT, SQ[t]], [pd], start=(t == 0), stop=(t == 7))
        RSQ(rstdT.ap[:, 0:N], pd.ap[:, 0:N], 1.0 / D, [pd], [rstdT])
        for t in range(8):
            tmp = nxt(TMPF, "tmp")
            STT(tmp.ap[:, 0:N], MG[t].ap[:, 0:N], vc(l, gain_name, t, t + 1), rstdT.ap[:, 0:N], ALU.mult, ALU.mult, [MG[t], vecT, rstdT], [tmp])
            TT("pool", X[t].ap[:, 0:N], X[t].ap[:, 0:N], tmp.ap[:, 0:N], ALU.add, [X[t], tmp], [X[t]])

    qF = tl([128, TB], BF16)
    kF = tl([128, TB], BF16)
    qdF = tl([128, TB], BF16)
    kdF = tl([128, TB], BF16)
    vF = [tl([128, TB], BF16) for _ in range(2)]
    gF = [tl([128, TB], BF16) for _ in range(2)]
    f1 = tl([128, TB], F32)
    f2 = tl([128, TB], F32)
    f3 = tl([128, TB], F32)
    f4 = tl([128, TB], F32)
    f5 = TMPF[0]
    eGh = TMPF[1]
    UE = [tl([128, 3 + TB], F32) for _ in range(2)]
    kdTM = tl([64, 1024], BF16)
    vTMb = tl([64, 1024], BF16)
    attW = tl([64, 512], BF16)
    onb = tl([64, 1024], BF16)
    xwb = tl([64, 512], BF16)
    SbC = tl([128, 8 * 256], BF16)
    CBs = tl([64, 128], F32)
    ubig = tl([128, 256], BF16)
    stA = tl([64, 16], F32)
    stLN = tl([64, 40], F32)
    st6 = [tl([64, 8], F32) for _ in range(4)]
    dtA = tl([64, NCH * 16], F32)
    gA = tl([64, NCH * 16], F32)
    GA = tl([64, NCH * 16], F32)
    eGA = tl([64, NCH * 16], F32)
    wjA = tl([64, NCH * 16], F32)
    eGtotA = tl([128, NCH * 16], F32)
    GFA = tl([16, NCH * C], F32)
    Zt = [tl([16, 256], F32) for _ in range(2)]
    ubuf = [tl([128, 64], BF16) for _ in range(2)]

    rk = {"i": 0}
    MARKS = []

    def mark(label):
        MARKS.append((label, len(P.ops["pe"])))

    def alt(lst):
        rk["i"] += 1
        return lst[rk["i"] % len(lst)]

    def retention_block(l, N):
        nch = N // C
        for h in range(4):
            for (c0, dst) in ((h * 128, qF), (512 + h * 128, kF)):
                w1, w1v = LW(w_in[l][:, c0:c0 + 128], 8, 128)
                p1 = proj(w1, w1v, 0, XN, N)
                w2, w2v = LW(None, 8, 128, parts=[(0, 64, w_in[l][:, c0 + 64:c0 + 128]), (64, 64, w_in[l][:, c0:c0 + 64])])
                p2 = proj(w2, w2v, 0, XN, N)
                TT("dve", f1.ap[:, 0:N], p1.ap[:, 0:N], cosT.ap[:, 0:N], ALU.mult, [p1, cosT], [f1])
                TT("dve", f2.ap[:, 0:N], p2.ap[:, 0:N], sinT.ap[:, 0:N], ALU.mult, [p2, sinT], [f2])
                TT("dve", dst.ap[:, 0:N], f1.ap[:, 0:N], f2.ap[:, 0:N], ALU.add, [f1, f2], [dst])
            TT(DBG.get("bceng", "pool"), qdF.ap[:, 0:N].rearrange("p (c i) -> p c i", i=C), qF.ap[:, 0:N].rearrange("p (c i) -> p c i", i=C),
               cf("gq", h * C, (h + 1) * C).unsqueeze(1).to_broadcast([128, nch, C]), ALU.mult, [qF, cstT], [qdF])
            wv_, wvv = LW(w_in[l][:, 1024 + h * 256:1024 + (h + 1) * 256], 8, 256)
            wg_, wgv = LW(w_in[l][:, 2048 + h * 256:2048 + (h + 1) * 256], 8, 256)
            for j in range(2):
                p = proj(wv_, wvv, j * 128, XN, N)
                CP("act", vF[j].ap[:, 0:N], p.ap[:, 0:N], [p], [vF[j]])
                p = proj(wg_, wgv, j * 128, XN, N)
                ACT(gF[j].ap[:, 0:N], p.ap[:, 0:N], AF.Silu, [p], [gF[j]])
            S = Sret[l][h]
            Sb = SretB[l][h]
            LV = DBG.get("lv", 9)
            if LV < 1:
                continue
            for c in range(nch):
                cs = slice(c * C, (c + 1) * C)
                kd = alt(kdT)
                vt = alt(vT)
                pt = nxt(PT, "pt")
                TR(pt.ap[0:C, 0:128], kF.ap[:, cs], ident_b, [kF, cbT], [pt])
                if LV < 2:
                    continue
                ACT(kd.ap, pt.ap[0:C, 0:128], AF.Copy, [pt, cstT], [kd], scale=cf("kds", h, h + 1)[0:C, :])
                for j in range(2):
                    pt = nxt(PT, "pt")
                    TR(pt.ap[0:C, 0:128], vF[j].ap[:, cs], ident_b, [vF[j], cbT], [pt])
                    CP("act", vt.ap[:, j * 128:(j + 1) * 128], pt.ap[0:C, 0:128], [pt], [vt])
                if LV < 3:
                    continue
                pa = nxt(PA, "pa")
                MM(pa.ap[0:C, 0:C], kF.ap[:, cs], qF.ap[:, cs], [kF, qF], [pa])
                am = alt(attm)
                TT("dve", am.ap[:, 0:C], pa.ap[0:C, 0:C], cf("maskret", h * C, (h + 1) * C)[0:C, :], ALU.mult, [pa, cstT], [am])
                if LV < 4:
                    continue
                po = nxt(PO, "po")
                MM(po.ap[0:C, 0:256], am.ap[:, 0:C], vt.ap, [am, vt], [po], start=True, stop=False)
                MM(po.ap[0:C, 0:256], qdF.ap[:, cs], Sb.ap, [qdF, Sb], [po], start=False, stop=True)
                ps = nxt(PS, "ps")
                MM(ps.ap[:, 0:256], kd.ap, vt.ap, [kd, vt], [ps])
                STT(S.ap, S.ap, cf("gamC", h, h + 1), ps.ap[:, 0:256], ALU.mult, ALU.add, [S, cstT, ps], [S])
                CP("act", Sb.ap, S.ap, [S], [Sb])
                if LV < 5:
                    continue
                s6 = alt(st6)
                P.op("dve", lambda e, s6=s6, po=po: e.bn_stats(out=s6.ap[:, 0:6], in_=po.ap[0:C, 0:256]), [po], [s6])
                P.op("dve", lambda e, s6=s6: e.bn_aggr(out=s6.ap[:, 6:8], in_=s6.ap[:, 0:6]), [s6], [s6])
                ACT(s6.ap[:, 7:8], s6.ap[:, 7:8], AF.Sqrt, [s6], [s6], bias=EPS)
                P.op("dve", lambda e, s6=s6: e.reciprocal(out=s6.ap[:, 7:8], in_=s6.ap[:, 7:8]), [s6], [s6])
                STT(s6.ap[:, 6:7], s6.ap[:, 6:7], -1.0, s6.ap[:, 7:8], ALU.mult, ALU.mult, [s6], [s6])
                if LV < 6:
                    continue
                on = alt(onT)
                ACT(on.ap, po.ap[0:C, 0:256], AF.Identity, [po, s6], [on], scale=s6.ap[:, 7:8], bias=s6.ap[:, 6:7])
                if LV < 7:
                    continue
                for j in range(2):
                    pt = nxt(PT, "pt")
                    TR(pt.ap[:, 0:C], on.ap[:, j * 128:(j + 1) * 128], ident_b[0:C, 0:C], [on, cbT], [pt])
                    STT(YB[h * 2 + j].ap[:, cs], pt.ap[:, 0:C], vc(l, "ret_norm", h * 2 + j, h * 2 + j + 1), gF[j].ap[:, cs], ALU.mult, ALU.mult,
                        [pt, vecT, gF[j]], [YB[h * 2 + j]])

    def hgrn_block(l, N):
        nch = N // C
        v3 = lambda ap: ap[:, 0:N].rearrange("p (c i) -> p c i", i=C)
        for h in range(8):
            wq, wqv = LW(w_in[l][:, 3072 + h * 128:3072 + (h + 1) * 128], 8, 128)
            p = proj(wq, wqv, 0, XN, N)
            ACT(f1.ap[:, 0:N], p.ap[:, 0:N], AF.Silu, [p], [f1])
            wf, wfv = LW(w_in[l][:, 4096 + h * 128:4096 + (h + 1) * 128], 8, 128)
            p = proj(wf, wfv, 0, XN, N)
            ACT(f2.ap[:, 0:N], p.ap[:, 0:N], AF.Sigmoid, [p], [f2])
            TS("dve", f2.ap[:, 0:N], f2.ap[:, 0:N], omlT.ap[:, l, h:h + 1], lbT.ap[:, l, h:h + 1], ALU.mult, ALU.add, [f2, omlT, lbT], [f2])
            TS("dve", f2.ap[:, 0:N], f2.ap[:, 0:N], 1e-6, 1.0, ALU.max, ALU.min, [f2], [f2])
            ACT(f3.ap[:, 0:N], f2.ap[:, 0:N], AF.Ln, [f2], [f3])
            TS("dve", f2.ap[:, 0:N], f2.ap[:, 0:N], -1.0, 1.0, ALU.mult, ALU.add, [f2], [f2])
            P.op("dve", lambda e: e.tensor_tensor_scan(out=f4.ap[:, 0:N], data0=cf("reset")[:, 0:N], data1=f3.ap[:, 0:N], initial=0.0, op0=ALU.mult, op1=ALU.add),
                 [f3, cstT], [f4])
            TT("dve", v3(f3.ap), v3(f4.ap), v3(f4.ap)[:, :, 31:32].to_broadcast([128, nch, C]), ALU.subtract, [f4], [f3])
            ACT(f5.ap[:, 0:N], f3.ap[:, 0:N], AF.Exp, [f3], [f5])
            TT("dve", qF.ap[:, 0:N], f1.ap[:, 0:N], f5.ap[:, 0:N], ALU.mult, [f1, f5], [qF])
            ACT(f5.ap[:, 0:N], f3.ap[:, 0:N], AF.Exp, [f3], [f5], scale=-1.0)
            TT("dve", kF.ap[:, 0:N], f2.ap[:, 0:N], f5.ap[:, 0:N], ALU.mult, [f2, f5], [kF])
            TT("dve", v3(f3.ap), v3(f4.ap)[:, :, C - 1:C].to_broadcast([128, nch, C]), v3(f4.ap), ALU.subtract, [f4], [f3])
            ACT(f3.ap[:, 0:N], f3.ap[:, 0:N], AF.Exp, [f3], [f3])
            TT("dve", kdF.ap[:, 0:N], f2.ap[:, 0:N], f3.ap[:, 0:N], ALU.mult, [f2, f3], [kdF])
            ACT(f4.ap[:, 0:N], f4.ap[:, 0:N], AF.Exp, [f4], [f4])
            TT("dve", qdF.ap[:, 0:N], f1.ap[:, 0:N], f4.ap[:, 0:N], ALU.mult, [f1, f4], [qdF])
            wv_, wvv = LW(w_in[l][:, 5120 + h * 128:5120 + (h + 1) * 128], 8, 128)
            p = proj(wv_, wvv, 0, XN, N)
            CP("act", vF[0].ap[:, 0:N], p.ap[:, 0:N], [p], [vF[0]])
            wt_, wtv = LW(w_in[l][:, 6144 + h * 128:6144 + (h + 1) * 128], 8, 128)
            p = proj(wt_, wtv, 0, XN, N)
            ACT(gF[0].ap[:, 0:N], p.ap[:, 0:N], AF.Sigmoid, [p], [gF[0]])
            S = Shg[l][h]
            Sb = ShgB[l][h]
            for c in range(nch):
                cs = slice(c * C, (c + 1) * C)
                kd = alt(kdT)
                vt = alt(vT)
                pt = nxt(PT, "pt")
                TR(pt.ap[0:C, 0:128], kdF.ap[:, cs], ident_b, [kdF, cbT], [pt])
                CP("act", kd.ap, pt.ap[0:C, 0:128], [pt], [kd])
                pt = nxt(PT, "pt")
                TR(pt.ap[0:C, 0:128], vF[0].ap[:, cs], ident_b, [vF[0], cbT], [pt])
                CP("act", vt.ap[:, 0:128], pt.ap[0:C, 0:128], [pt], [vt])
                pa = nxt(PA, "pa")
                MM(pa.ap[0:C, 0:C], kF.ap[:, cs], qF.ap[:, cs], [kF, qF], [pa])
                am = alt(attm)
                TT("dve", am.ap[:, 0:C], pa.ap[0:C, 0:C], cf("mask01")[0:C, :], ALU.mult, [pa, cstT], [am])
                po = nxt(PO, "po")
                MM(po.ap[0:C, 0:128], am.ap[:, 0:C], vt.ap[:, 0:128], [am, vt], [po], start=True, stop=False)
                MM(po.ap[0:C, 0:128], qdF.ap[:, cs], Sb.ap, [qdF, Sb], [po], start=False, stop=True)
                ps = nxt(PS, "ps")
                MM(ps.ap[:, 0:128], kd.ap, vt.ap[:, 0:128], [kd, vt], [ps])
                STT(S.ap, S.ap, f4.ap[:, c * C + C - 1:c * C + C], ps.ap[:, 0:128], ALU.mult, ALU.add, [S, f4, ps], [S])
                CP("act", Sb.ap, S.ap, [S], [Sb])
                s6 = alt(st6)
                on = alt(onT)
                P.op("act", lambda e, on=on, po=po, s6=s6: e.activation(out=on.ap[:, 128:256], in_=po.ap[0:C, 0:128], func=AF.Square, accum_out=s6.ap[:, 0:1]),
                     [po], [on, s6])
                ACT(s6.ap[:, 1:2], s6.ap[:, 0:1], AF.Sqrt, [s6], [s6], scale=1.0 / 128, bias=EPS)
                P.op("dve", lambda e, s6=s6: e.reciprocal(out=s6.ap[:, 1:2], in_=s6.ap[:, 1:2]), [s6], [s6])
                ACT(on.ap[:, 0:128], po.ap[0:C, 0:128], AF.Copy, [po, s6], [on], scale=s6.ap[:, 1:2])
                pt = nxt(PT, "pt")
                TR(pt.ap[:, 0:C], on.ap[:, 0:128], ident_b[0:C, 0:C], [on, cbT], [pt])
                STT(YB[h].ap[:, cs], pt.ap[:, 0:C], vc(l, "hg_norm", h, h + 1), gF[0].ap[:, cs], ALU.mult, ALU.mult, [pt, vecT, gF[0]], [YB[h]])


    def tr_in(src_list, dst_T, dst_cols, scale_ap=None):
        n = len(src_list)
        for i, (tt, ap) in enumerate(src_list):
            TR(psB[0:C, i * 128:(i + 1) * 128], ap, ident_b, [tt, cbT], [PTr])
        if scale_ap is None:
            CP("act", dst_T.ap[:, dst_cols[0]:dst_cols[0] + n * 128], psB[0:C, 0:n * 128], [PTr], [dst_T])
        else:
            ACT(dst_T.ap[:, dst_cols[0]:dst_cols[0] + n * 128], psB[0:C, 0:n * 128], AF.Copy, [PTr, cstT], [dst_T], scale=scale_ap)

    def gla_half(B, l, h, c0, HC, dv, S, Sb, decay_fn, mask_ap, gain_name, ytile0, gates, kd_scale=None, norm="rms"):
        nvt = dv // 128
        T0 = c0 * C
        W = HC * C
        tr_in([(B.kdF, B.kdF.ap[:, (c0 + c) * C:(c0 + c + 1) * C]) for c in range(HC)], kdTM, (0,), kd_scale)
        for j in range(nvt):
            for c in range(HC):
                TR(psB[0:C, c * 128:(c + 1) * 128], B.vF[j].ap[:, (c0 + c) * C:(c0 + c + 1) * C], ident_b, [B.vF[j], cbT], [PTr])
            CP("act", vTMb.ap[:, 0:HC * dv].rearrange("p (c v) -> p c v", v=dv)[:, :, j * 128:(j + 1) * 128],
               psB[0:C, 0:HC * 128].rearrange("p (c v) -> p c v", v=128), [PTr], [vTMb])
        yield
        for c in range(HC):
            cs = slice((c0 + c) * C, (c0 + c + 1) * C)
            MM(psF[0:C, 2, c * C:(c + 1) * C], B.kF.ap[:, cs], B.qF.ap[:, cs], [B.kF, B.qF], [PAr])
        TT("dve", attW.ap[:, 0:HC * C].rearrange("p (c i) -> p c i", i=C), psF[0:C, 2, 0:HC * C].rearrange("p (c i) -> p c i", i=C),
           mask_ap.unsqueeze(1).to_broadcast([C, HC, C]), ALU.mult, [PAr, cstT], [attW])
        yield
        for c in range(HC):
            MM(psv(3, c * dv, dv), kdTM.ap[:, c * 128:(c + 1) * 128], vTMb.ap[:, c * dv:(c + 1) * dv], [kdTM, vTMb], [PSr])
        yield
        cur = S
        for c in range(HC):
            nx = Spp if cur is S else S
            STT(nx.ap[:, 0:dv], cur.ap[:, 0:dv], decay_fn(c0 + c), psv(3, c * dv, dv), ALU.mult, ALU.add, [cur, PSr] + decay_reads, [nx])
            CP("act", SbC.ap[:, (c0 + c) * dv:(c0 + c + 1) * dv], nx.ap[:, 0:dv], [nx], [SbC])
            cur = nx
            yield
        assert cur is S
        for c in range(HC):
            cs = slice((c0 + c) * C, (c0 + c + 1) * C)
            MM(psv(5, c * dv, dv, C), attW.ap[:, c * C:(c + 1) * C], vTMb.ap[:, c * dv:(c + 1) * dv], [attW, vTMb], [POr], start=True, stop=False)
            if c0 + c == 0:
                MM(psv(5, c * dv, dv, C), B.qdF.ap[:, cs], Sb.ap, [B.qdF, Sb], [POr], start=False, stop=True)
            else:
                MM(psv(5, c * dv, dv, C), B.qdF.ap[:, cs], SbC.ap[:, (c0 + c - 1) * dv:(c0 + c) * dv], [B.qdF, SbC], [POr], start=False, stop=True)
        if c0 + HC == NCH:
            CP("act", Sb.ap, SbC.ap[:, (NCH - 1) * dv:NCH * dv], [SbC], [Sb])
        yield
        if norm == "ln":
            for c in range(HC):
                P.op("dve", lambda e, c=c: e.bn_stats(out=stLN.ap[:, c * 6:(c + 1) * 6], in_=psv(5, c * dv, dv, C)), [POr], [stLN])
            for c in range(HC):
                P.op("dve", lambda e, c=c: e.bn_aggr(out=stLN.ap[:, 24 + 2 * c:26 + 2 * c], in_=stLN.ap[:, c * 6:(c + 1) * 6]), [stLN], [stLN])
            mv = stLN.ap[:, 24:24 + 2 * HC].rearrange("p (c t) -> p c t", t=2)
            RSQ(stLN.ap[:, 32:32 + HC], mv[:, :, 1], 1.0, [stLN], [stLN])
            STT(stLN.ap[:, 36:36 + HC], mv[:, :, 0], -1.0, stLN.ap[:, 32:32 + HC], ALU.mult, ALU.mult, [stLN], [stLN])
            for c in range(HC):
                ACT(onb.ap[:, c * dv:(c + 1) * dv], psv(5, c * dv, dv, C), AF.Identity, [POr, stLN], [onb], scale=stLN.ap[:, 32 + c:33 + c], bias=stLN.ap[:, 36 + c:37 + c])
        npc = ((HC * dv) // 512 if HC * dv > 512 else 1) if norm != "ln" else 0
        pw = (HC * dv) // max(npc, 1)
        cpp = pw // dv
        for pc in range(npc):
            po_ap = psF[0:C, 5 + pc, 0:pw] if npc > 1 else psF[0:C, 5, 0:pw]
            po3 = po_ap.rearrange("p (c v) -> p c v", v=dv)
            sq = bs1.ap[0:C, 0:pw]
            ACT(sq, po_ap, AF.Square, [POr], [bs1])
            P.op("dve", lambda e, sq=sq: e.tensor_reduce(out=stA.ap[:, 0:cpp], in_=sq.rearrange("p (c v) -> p c v", v=dv), axis=mybir.AxisListType.X, op=ALU.add),
                 [bs1], [stA])
            if norm == "rms":
                RSQ(stA.ap[:, 0:cpp], stA.ap[:, 0:cpp], 1.0 / dv, [stA], [stA])
                TT("dve", onb.ap[:, pc * pw:(pc + 1) * pw].rearrange("p (c v) -> p c v", v=dv), po3, stA.ap[:, 0:cpp].unsqueeze(2).to_broadcast([C, cpp, dv]),
                   ALU.mult, [POr, stA], [onb])
            else:
                pass
        yield
        for j in range(nvt):
            for c in range(HC):
                TR(psB[:, c * C:(c + 1) * C], onb.ap[:, c * dv + j * 128:c * dv + (j + 1) * 128], ident_b[0:C, 0:C], [onb, cbT], [PTr])
            yt = YB[ytile0 + j]
            STT(yt.ap[:, T0:T0 + W], psB[:, 0:W], vc(l, gain_name, ytile0 + j, ytile0 + j + 1), gates[j].ap[:, T0:T0 + W], ALU.mult, ALU.mult,
                [PTr, vecT, gates[j]], [yt])
            yield

    decay_reads = []
    Spp = tl([128, 256], F32)

    def ret_front(B, l, N, h):
        nch = N // C
        for (c0, dst) in ((h * 128, B.qF), (512 + h * 128, B.kF)):
            w1, w1v = LW(None, 8, 256, parts=[(0, 128, w_in[l][:, c0:c0 + 128]), (128, 64, w_in[l][:, c0 + 64:c0 + 128]), (192, 64, w_in[l][:, c0:c0 + 64])])
            p1 = proj(w1, w1v, 0, XN, N)
            yield
            p2 = proj(w1, w1v, 128, XN, N)
            yield
            TT("dve", f1.ap[:, 0:N], p1.ap[:, 0:N], cosT.ap[:, 0:N], ALU.mult, [p1, cosT], [f1])
            TT("dve", f2.ap[:, 0:N], p2.ap[:, 0:N], sinT.ap[:, 0:N], ALU.mult, [p2, sinT], [f2])
            yield
            TT("dve", dst.ap[:, 0:N], f1.ap[:, 0:N], f2.ap[:, 0:N], ALU.add, [f1, f2], [dst])
            yield
        TT("pool", B.qdF.ap[:, 0:N].rearrange("p (c i) -> p c i", i=C), B.qF.ap[:, 0:N].rearrange("p (c i) -> p c i", i=C),
           cf("gq", h * C, (h + 1) * C).unsqueeze(1).to_broadcast([128, nch, C]), ALU.mult, [B.qF, cstT], [B.qdF])
        CP("pool", B.kdF.ap[:, 0:N], B.kF.ap[:, 0:N], [B.kF], [B.kdF])
        yield
        wv_, wvv = LW(w_in[l][:, 1024 + h * 256:1024 + (h + 1) * 256], 8, 256)
        wg_, wgv = LW(w_in[l][:, 2048 + h * 256:2048 + (h + 1) * 256], 8, 256)
        for j in range(2):
            p = proj(wv_, wvv, j * 128, XN, N)
            CP("act", B.vF[j].ap[:, 0:N], p.ap[:, 0:N], [p], [B.vF[j]])
            yield
            p = proj(wg_, wgv, j * 128, XN, N)
            SILU(B.gF[j], p, N)
            yield

    def ret_back(B, l, N, h):
        nch = N // C
        decay_reads[:] = [cstT]
        for c0 in range(0, nch, 4):
            decay_reads[:] = [cstT]
            yield from gla_half(B, l, h, c0, 4, 256, Sret[l][h], SretB[l][h], lambda c, h=h: cf("gamC", h, h + 1), cf("maskret", h * C, (h + 1) * C)[0:C, :],
                                "ret_norm", 2 * h, B.gF, kd_scale=cf("kds", h, h + 1)[0:C, :], norm="ln")

    def hg_front(B, l, N, h):
        nch = N // C
        v3 = lambda ap: ap[:, 0:N].rearrange("p (c i) -> p c i", i=C)
        wq, wqv = LW(w_in[l][:, 3072 + h * 128:3072 + (h + 1) * 128], 8, 128)
        p = proj(wq, wqv, 0, XN, N)
        SILU(f1, p, N)
        yield
        wf, wfv = LW(w_in[l][:, 4096 + h * 128:4096 + (h + 1) * 128], 8, 128)
        p = proj(wf, wfv, 0, XN, N)
        ACT(f2.ap[:, 0:N], p.ap[:, 0:N], AF.Sigmoid, [p], [f2])
        yield
        wt_, wtv = LW(w_in[l][:, 6144 + h * 128:6144 + (h + 1) * 128], 8, 128)
        p = proj(wt_, wtv, 0, XN, N)
        ACT(B.gF[0].ap[:, 0:N], p.ap[:, 0:N], AF.Sigmoid, [p], [B.gF[0]])
        yield
        wv_, wvv = LW(w_in[l][:, 5120 + h * 128:5120 + (h + 1) * 128], 8, 128)
        p = proj(wv_, wvv, 0, XN, N)
        CP("act", B.vF[0].ap[:, 0:N], p.ap[:, 0:N], [p], [B.vF[0]])
        yield
        TS("dve", f2.ap[:, 0:N], f2.ap[:, 0:N], omlT.ap[:, l, h:h + 1], lbT.ap[:, l, h:h + 1], ALU.mult, ALU.add, [f2, omlT, lbT], [f2])
        TS("dve", f2.ap[:, 0:N], f2.ap[:, 0:N], 1e-6, 1.0, ALU.max, ALU.min, [f2], [f2])
        yield
        ACT(f3.ap[:, 0:N], f2.ap[:, 0:N], AF.Ln, [f2], [f3])
        TS("dve", f2.ap[:, 0:N], f2.ap[:, 0:N], -1.0, 1.0, ALU.mult, ALU.add, [f2], [f2])
        yield
        P.op("dve", lambda e: e.tensor_tensor_scan(out=f4.ap[:, 0:N], data0=reset_b[:, 0:N], data1=f3.ap[:, 0:N], initial=0.0, op0=ALU.mult, op1=ALU.add),
             [f3, cbT], [f4])
        yield
        TT("dve", v3(f3.ap), v3(f4.ap), v3(f4.ap)[:, :, 31:32].to_broadcast([128, nch, C]), ALU.subtract, [f4], [f3])
        yield
        ACT(f5.ap[:, 0:N], f3.ap[:, 0:N], AF.Exp, [f3], [f5])
        TT("dve", B.qF.ap[:, 0:N], f1.ap[:, 0:N], f5.ap[:, 0:N], ALU.mult, [f1, f5], [B.qF])
        yield
        ACT(f5.ap[:, 0:N], f3.ap[:, 0:N], AF.Exp, [f3], [f5], scale=-1.0)
        TT("dve", B.kF.ap[:, 0:N], f2.ap[:, 0:N], f5.ap[:, 0:N], ALU.mult, [f2, f5], [B.kF])
        TT("dve", v3(f3.ap), v3(f4.ap)[:, :, C - 1:C].to_broadcast([128, nch, C]), v3(f4.ap), ALU.subtract, [f4], [f3])
        yield
        ACT(f3.ap[:, 0:N], f3.ap[:, 0:N], AF.Exp, [f3], [f3])
        yield
        TT("dve", B.kdF.ap[:, 0:N], f2.ap[:, 0:N], f3.ap[:, 0:N], ALU.mult, [f2, f3], [B.kdF])
        ACT(B.eG.ap[:, 0:N], f4.ap[:, 0:N], AF.Exp, [f4], [B.eG])
        yield
        TT("dve", B.qdF.ap[:, 0:N], f1.ap[:, 0:N], B.eG.ap[:, 0:N], ALU.mult, [f1, B.eG], [B.qdF])
        yield

    def hg_back(B, l, N, h):
        nch = N // C
        for c0 in range(0, nch, nch):
            decay_reads[:] = [B.eG]
            yield from gla_half(B, l, h, c0, nch, 128, Shg[l][h], ShgB[l][h], lambda c: B.eG.ap[:, c * C + C - 1:c * C + C], cf("mask01")[0:C, :],
                                "hg_norm", h, B.gF, norm="rms")

    def ssd_dt_prep2(l, N):
        nch = N // C
        wd, wdv = LW(w_in[l][:, 10240:10256], 8, 16)
        for c in range(nch):
            cs = slice(c * C, (c + 1) * C)
            for kt in range(8):
                MM(psF[0:C, 2, c * 16:(c + 1) * 16], XN[kt].ap[:, cs], wdv[:, kt, :], [XN[kt], wd], [PAr], start=(kt == 0), stop=(kt == 7))
        W16 = nch * 16
        c3 = lambda ap: ap.rearrange("p (c h) -> p c h", h=16)
        TT("dve", c3(dtA.ap[:, 0:W16]), c3(psF[0:C, 2, 0:W16]), vc(l, "dt_bias", 0, 16)[0:C, :].unsqueeze(1).to_broadcast([C, nch, 16]), ALU.add, [PAr, vecT], [dtA])
        ACT(dtA.ap[:, 0:W16], dtA.ap[:, 0:W16], AF.Exp, [dtA], [dtA])
        ACT(dtA.ap[:, 0:W16], dtA.ap[:, 0:W16], AF.Ln, [dtA], [dtA], bias=1.0)
        TT("dve", c3(gA.ap[:, 0:W16]), c3(dtA.ap[:, 0:W16]), AnT.ap[0:C, l, :].unsqueeze(1).to_broadcast([C, nch, 16]), ALU.mult, [dtA, AnT], [gA])
        MM(psF[0:C, 2, 0:W16], cf("mask01")[0:C, :], gA.ap[:, 0:W16], [cstT, gA], [PAr])
        CP("dve", GA.ap[:, 0:W16], psF[0:C, 2, 0:W16], [PAr], [GA])
        ACT(eGA.ap[:, 0:W16], psF[0:C, 2, 0:W16], AF.Exp, [PAr], [eGA])
        MM(psF[:, 2, 0:W16], ones_f[0:C, :], gA.ap[:, 0:W16], [cstT, gA], [PAr])
        ACT(eGtotA.ap[:, 0:W16], psF[:, 2, 0:W16], AF.Exp, [PAr], [eGtotA])
        TT("dve", wjA.ap[:, 0:W16], psF[0:C, 2, 0:W16], GA.ap[:, 0:W16], ALU.subtract, [PAr, GA], [wjA])
        ACT(wjA.ap[:, 0:W16], wjA.ap[:, 0:W16], AF.Exp, [wjA], [wjA])
        TT("dve", wjA.ap[:, 0:W16], wjA.ap[:, 0:W16], dtA.ap[:, 0:W16], ALU.mult, [wjA, dtA], [wjA])
        for c in range(nch):
            MM(psF[0:16, 2, c * C:(c + 1) * C], gA.ap[:, c * 16:(c + 1) * 16], cf("mask01")[0:C, :], [gA, cstT], [PAr])
        CP("dve", GFA.ap[:, 0:nch * C], psF[0:16, 2, 0:nch * C], [PAr], [GFA])

    def ssd_front(B, l, N, g):
        nch = N // C
        tiles = [(2 * g, 8192 + g * 256, 256, 0, B.vF[0]), (2 * g + 1, None, None, 128, B.vF[1]), (8 + g, 9216 + g * 128, 128, 0, B.kF), (12 + g, 9728 + g * 128, 128, 0, B.qF)]
        for k, (ct, wc0, wn, m0, dst) in enumerate(tiles):
            ue = UE[k % 2]
            if wc0 is not None:
                wt_, wv_ = LW(w_in[l][:, wc0:wc0 + wn], 8, wn)
            p = proj(wt_, wv_, m0, XN, N)
            CP("dve", ue.ap[:, 0:3], HALO[l][ct].ap, [HALO[l][ct]], [ue])
            CP("act", ue.ap[:, 3:3 + N], p.ap[:, 0:N], [p], [ue])
            CP("dve", HALO[l][ct].ap, ue.ap[:, N:N + 3], [ue], [HALO[l][ct]])
            yield
            conv_tile(l, ct, ue, N, dst)
            yield
        wz, wzv = LW(w_in[l][:, 7168 + g * 256:7168 + (g + 1) * 256], 8, 256)
        for j in range(2):
            p = proj(wz, wzv, j * 128, XN, N)
            SILU(B.gF[j], p, N)
            yield

    def ssd_back(B, l, N, g):
        nch = N // C
        HC = 2
        S = Sssm[l][g]
        Sb = SssmB[l][g]
        hs = slice(4 * g, 4 * g + 4)
        h4 = lambda t: t.ap.rearrange("p (c h) -> p c h", h=16)
        for c0 in range(0, nch, HC):
            T0 = c0 * C
            W = HC * C
            tr_in([(B.kF, B.kF.ap[:, (c0 + c) * C:(c0 + c + 1) * C]) for c in range(HC)], kdTM, (0,))
            for j in range(2):
                for c in range(HC):
                    TR(psB[0:C, (j * HC + c) * 128:(j * HC + c + 1) * 128], B.vF[j].ap[:, (c0 + c) * C:(c0 + c + 1) * C], ident_b, [B.vF[j], cbT], [PTr])
            for j in range(2):
                CP("act", vTMb.ap[:, 0:HC * 256].rearrange("p (c v) -> p c v", v=256)[:, :, j * 128:(j + 1) * 128],
                   psB[0:C, j * HC * 128:(j + 1) * HC * 128].rearrange("p (c v) -> p c v", v=128), [PTr], [vTMb])
            xs4 = vTMb.ap[:, 0:HC * 256].rearrange("p (c h d) -> p c h d", h=4, d=64)
            yield
            for c in range(HC):
                cs = slice((c0 + c) * C, (c0 + c + 1) * C)
                MM(psF[0:C, 2, c * C:(c + 1) * C], B.kF.ap[:, cs], B.qF.ap[:, cs], [B.kF, B.qF], [PAr])
            CP("act", CBs.ap[:, 0:HC * C], psF[0:C, 2, 0:HC * C], [PAr], [CBs])
            yield
            for c in range(HC):
                zt = alt(Zt)
                TT("pool", zt.ap.rearrange("p (h i) -> p h i", i=C), GFA.ap[:, (c0 + c) * C:(c0 + c + 1) * C].unsqueeze(1).to_broadcast([16, 4, C]),
                   cf("eye16p", 4 * g, 4 * g + 4)[0:16, :].unsqueeze(2).to_broadcast([16, 4, C]), ALU.mult, [GFA, cstT], [zt])
                MM(psF[0:C, 3, c * 256:(c + 1) * 256], ones_f[0:16, 0:C], zt.ap, [cstT, zt], [PSr])
            TT("pool", onb.ap[:, 0:HC * 256].rearrange("p (c h i) -> p c h i", h=4, i=C),
               CBs.ap[:, 0:HC * C].rearrange("p (c i) -> p c i", i=C).unsqueeze(2).to_broadcast([C, HC, 4, C]),
               h4(dtA)[:, c0:c0 + HC, hs].unsqueeze(3).to_broadcast([C, HC, 4, C]), ALU.mult, [CBs, dtA], [onb])
            TT("pool", onb.ap[:, 512:512 + HC * 256].rearrange("p (c h d) -> p c h d", h=4, d=64), xs4,
               vc(l, "d_skip", 4 * g, 4 * g + 4)[0:C, :].unsqueeze(1).unsqueeze(3).to_broadcast([C, HC, 4, 64]), ALU.mult, [vTMb, vecT], [onb])
            TT("pool", xwb.ap[:, 0:HC * 256].rearrange("p (c h d) -> p c h d", h=4, d=64), xs4,
               h4(wjA)[:, c0:c0 + HC, hs].unsqueeze(3).to_broadcast([C, HC, 4, 64]), ALU.mult, [vTMb, wjA], [xwb])
            L4 = f3.ap[0:C, 0:HC * 256].rearrange("p (c h i) -> p c h i", h=4, i=C)
            TT("dve", L4, psF[0:C, 3, 0:HC * 256].rearrange("p (c h i) -> p c h i", h=4, i=C), h4(GA)[:, c0:c0 + HC, hs].unsqueeze(3).to_broadcast([C, HC, 4, C]),
               ALU.subtract, [PSr, GA], [f3])
            L3 = f3.ap[0:C, 0:HC * 256].rearrange("p (x i) -> p x i", i=C)
            TT("dve", L3, L3, cf("negmask")[0:C, :].unsqueeze(1).to_broadcast([C, HC * 4, C]), ALU.add, [f3, cstT], [f3])
            yield
            ACT(f3.ap[0:C, 0:HC * 256], f3.ap[0:C, 0:HC * 256], AF.Exp, [f3], [f3])
            yield
            TT("dve", attW.ap[:, 0:HC * 256], f3.ap[0:C, 0:HC * 256], onb.ap[:, 0:HC * 256], ALU.mult, [f3, onb], [attW])
            yield
            for c in range(HC):
                MM(psF[0:C, 5, c * 256:(c + 1) * 256], ident_b[0:C, 0:C], onb.ap[:, 512 + c * 256:512 + (c + 1) * 256], [cbT, onb], [POr], start=True, stop=False)
                for hh in range(4):
                    o_ = c * 256 + hh * 64
                    MM(psF[0:C, 5, o_:o_ + 64], attW.ap[:, c * 256 + hh * C:c * 256 + (hh + 1) * C], vTMb.ap[:, o_:o_ + 64], [attW, vTMb], [POr],
                       start=False, stop=(hh == 3))
            yield
            for c in range(HC):
                MM(psF[:, 4, c * 256:(c + 1) * 256], kdTM.ap[:, c * 128:(c + 1) * 128], xwb.ap[:, c * 256:(c + 1) * 256], [kdTM, xwb], [PSr])
            yield
            for c in range(HC):
                S3 = S.ap.rearrange("p (h d) -> p h d", d=64)
                TT("dve", S3, S3, eGtotA.ap[:, (c0 + c) * 16 + 4 * g:(c0 + c) * 16 + 4 * g + 4].unsqueeze(2).to_broadcast([128, 4, 64]), ALU.mult, [S, eGtotA], [S])
                TT("dve", S.ap, S.ap, psF[:, 4, c * 256:(c + 1) * 256], ALU.add, [S, PSr], [S])
                CP("act", SbC.ap[:, (c0 + c) * 256:(c0 + c + 1) * 256], S.ap, [S], [SbC])
            yield
            pdi = nxt(PD, "pd")
            for c in range(HC):
                cs = slice((c0 + c) * C, (c0 + c + 1) * C)
                rhs_ = Sb.ap if c0 + c == 0 else SbC.ap[:, (c0 + c - 1) * 256:(c0 + c) * 256]
                MM(pdi.ap[0:C, c * 256:(c + 1) * 256], B.qF.ap[:, cs], rhs_, [B.qF, Sb if c0 + c == 0 else SbC], [pdi])
            if c0 + HC == nch:
                CP("act", Sb.ap, SbC.ap[:, (nch - 1) * 256:nch * 256], [SbC], [Sb])
            o4 = f4.ap[0:C, 0:HC * 256].rearrange("p (c h d) -> p c h d", h=4, d=64)
            TT("dve", o4, pdi.ap[0:C, 0:HC * 256].rearrange("p (c h d) -> p c h d", h=4, d=64), h4(eGA)[:, c0:c0 + HC, hs].unsqueeze(3).to_broadcast([C, HC, 4, 64]),
               ALU.mult, [pdi, eGA], [f4])
            TT("dve", f4.ap[0:C, 0:HC * 256], f4.ap[0:C, 0:HC * 256], psF[0:C, 5, 0:HC * 256], ALU.add, [f4, POr], [f4])
            yield
            for j in range(2):
                for c in range(HC):
                    TR(psF[:, 2, (j * HC + c) * C:(j * HC + c + 1) * C], f4.ap[0:C, c * 256 + j * 128:c * 256 + (j + 1) * 128], ident_f[0:C, 0:C], [f4, cstT], [PAr])
            for j in range(2):
                TT("dve", ubig.ap[:, 0:W], psF[:, 2, j * W:(j + 1) * W], B.gF[j].ap[:, T0:T0 + W], ALU.mult, [PAr, B.gF[j]], [ubig])
                ACT(SQ[2 * g + j].ap[:, T0:T0 + W], ubig.ap[:, 0:W], AF.Square, [ubig], [SQ[2 * g + j]])
                TS("dve", YB[2 * g + j].ap[:, T0:T0 + W], ubig.ap[:, 0:W], vc(l, "ssm_norm", 2 * g + j, 2 * g + j + 1), None, ALU.mult, ALU.bypass,
                   [ubig, vecT], [YB[2 * g + j]])
        yield

    def ssd_tail(l, N):
        pd = nxt(PD, "pd")
        for t in range(8):
            MM(pd.ap[:, 0:N], ones_b, SQ[t].ap[:, 0:N], [cbT, SQ[t]], [pd], start=(t == 0), stop=(t == 7))
        RSQ(rstdM.ap[:, 0:N], pd.ap[:, 0:N], 1.0 / D, [pd], [rstdM])

    class BufSet:
        pass

    def interleave(g1, g2):
        gens = [x for x in (g1, g2) if x is not None]
        while gens:
            for x in list(gens):
                try:
                    next(x)
                except StopIteration:
                    gens.remove(x)

    def run_pipeline(l, N, front, back, n, first_done=False):
        if not first_done:
            for _ in front(IF[0], l, N, 0):
                pass
        for h in range(n):
            bg = back(IF[h % 2], l, N, h)
            fg = front(IF[(h + 1) % 2], l, N, h + 1) if h + 1 < n else None
            if DBG.get("nopipe"):
                for _ in bg:
                    pass
                if fg is not None:
                    for _ in fg:
                        pass
            else:
                interleave(bg, fg)

    def retention_block2(l, N):
        run_pipeline(l, N, ret_front, ret_back, 4)

    def hgrn_block2(l, N, first_done=False):
        run_pipeline(l, N, hg_front, hg_back, 8, first_done)

    def ssd_block2(l, N, first_done=False):
        if not first_done:
            ssd_dt_prep2(l, N)
        run_pipeline(l, N, ssd_front, ssd_back, 4, first_done)
        ssd_tail(l, N)

    def conv_tile(l, ct, ue, N, dst_bf, hist=None):
        cw = lambda w: vc(l, "conv_w", w * 16 + ct, w * 16 + ct + 1)
        cb = vc(l, "conv_b", ct, ct + 1)
        TS("dve", f1.ap[:, 0:N], ue.ap[:, 3:3 + N], cw(3), cb, ALU.mult, ALU.add, [ue, vecT], [f1])
        for w in (2, 1, 0):
            src = ue.ap[:, w:w + N] if hist is None else hist[0][:, :, w]
            rd = [ue, vecT, f1] if hist is None else [hist[1], vecT, f1]
            STT(f1.ap[:, 0:N], src, cw(w), f1.ap[:, 0:N], ALU.mult, ALU.add, rd, [f1])
        tmp = nxt(TMPF, "tmp")
        ACT(tmp.ap[:, 0:N], f1.ap[:, 0:N], AF.Sigmoid, [f1], [tmp])
        TT("dve", dst_bf.ap[:, 0:N], f1.ap[:, 0:N], tmp.ap[:, 0:N], ALU.mult, [f1, tmp], [dst_bf])

    def ssd_dt_prep(l, N):
        nch = N // C
        wd, wdv = LW(w_in[l][:, 10240:10256], 8, 16)
        for c in range(nch):
            cs = slice(c * C, (c + 1) * C)
            pa = nxt(PA, "pa")
            for kt in range(8):
                MM(pa.ap[0:C, 0:16], XN[kt].ap[:, cs], wdv[:, kt, :], [XN[kt], wd], [pa], start=(kt == 0), stop=(kt == 7))
            TT("dve", dtT[c].ap, pa.ap[0:C, 0:16], vc(l, "dt_bias", 0, 16)[0:C, :], ALU.add, [pa, vecT], [dtT[c]])
            ACT(dtT[c].ap, dtT[c].ap, AF.Exp, [dtT[c]], [dtT[c]])
            ACT(dtT[c].ap, dtT[c].ap, AF.Ln, [dtT[c]], [dtT[c]], bias=1.0)
            TT("dve", gT[c].ap, dtT[c].ap, AnT.ap[0:C, l, :], ALU.mult, [dtT[c], AnT], [gT[c]])
            pa = nxt(PA, "pa")
            MM(pa.ap[0:C, 0:16], cf("mask01")[0:C, :], gT[c].ap, [cstT, gT[c]], [pa])
            CP("dve", GT[c].ap, pa.ap[0:C, 0:16], [pa], [GT[c]])
            ACT(eGT[c].ap, pa.ap[0:C, 0:16], AF.Exp, [pa], [eGT[c]])
            pa2 = nxt(PA, "pa")
            MM(pa2.ap[:, 0:16], ones_f[0:C, :], gT[c].ap, [cstT, gT[c]], [pa2])
            ACT(eGtot[c].ap, pa2.ap[:, 0:16], AF.Exp, [pa2], [eGtot[c]])
            TT("dve", wjT[c].ap, pa2.ap[0:C, 0:16], GT[c].ap, ALU.subtract, [pa2, GT[c]], [wjT[c]])
            ACT(wjT[c].ap, wjT[c].ap, AF.Exp, [wjT[c]], [wjT[c]])
            TT("dve", wjT[c].ap, wjT[c].ap, dtT[c].ap, ALU.mult, [wjT[c], dtT[c]], [wjT[c]])
            pa3 = nxt(PA, "pa")
            MM(pa3.ap[0:16, 0:C], gT[c].ap, cf("mask01")[0:C, :], [gT[c], cstT], [pa3])
            CP("dve", GF[c].ap, pa3.ap[0:16, 0:C], [pa3], [GF[c]])

    def ssd_block(l, N):
        nch = N // C
        ssd_dt_prep(l, N)
        for g in range(4):
            tiles = [(2 * g, 8192 + g * 256, 256, 0, vF[0]), (2 * g + 1, None, None, 128, vF[1]), (8 + g, 9216 + g * 128, 128, 0, kF), (12 + g, 9728 + g * 128, 128, 0, qF)]
            for k, (ct, wc0, wn, m0, dst) in enumerate(tiles):
                ue = UE[k % 2]
                if wc0 is not None:
                    wt_, wv_ = LW(w_in[l][:, wc0:wc0 + wn], 8, wn)
                p = proj(wt_, wv_, m0, XN, N)
                CP("dve", ue.ap[:, 0:3], HALO[l][ct].ap, [HALO[l][ct]], [ue])
                CP("act", ue.ap[:, 3:3 + N], p.ap[:, 0:N], [p], [ue])
                CP("dve", HALO[l][ct].ap, ue.ap[:, N:N + 3], [ue], [HALO[l][ct]])
                conv_tile(l, ct, ue, N, dst)
            wz, wzv = LW(w_in[l][:, 7168 + g * 256:7168 + (g + 1) * 256], 8, 256)
            for j in range(2):
                p = proj(wz, wzv, j * 128, XN, N)
                ACT(gF[j].ap[:, 0:N], p.ap[:, 0:N], AF.Silu, [p], [gF[j]])
            S = Sssm[l][g]
            Sb = SssmB[l][g]
            hs = slice(4 * g, 4 * g + 4)
            for c in range(nch):
                cs = slice(c * C, (c + 1) * C)
                kd = alt(kdT)
                vt = alt(vT)
                pt = nxt(PT, "pt")
                TR(pt.ap[0:C, 0:128], kF.ap[:, cs], ident_b, [kF, cbT], [pt])
                CP("act", kd.ap, pt.ap[0:C, 0:128], [pt], [kd])
                for j in range(2):
                    pt = nxt(PT, "pt")
                    TR(pt.ap[0:C, 0:128], vF[j].ap[:, cs], ident_b, [vF[j], cbT], [pt])
                    CP("act", vt.ap[:, j * 128:(j + 1) * 128], pt.ap[0:C, 0:128], [pt], [vt])
                xd = alt(xdT)
                TT("dve", xd.ap.rearrange("p (h d) -> p h d", d=64), vt.ap.rearrange("p (h d) -> p h d", d=64),
                   vc(l, "d_skip", 4 * g, 4 * g + 4)[0:C, :].unsqueeze(2).to_broadcast([C, 4, 64]), ALU.mult, [vt, vecT], [xd])
                xw = alt(xwT)
                TT("dve", xw.ap.rearrange("p (h d) -> p h d", d=64), vt.ap.rearrange("p (h d) -> p h d", d=64),
                   wjT[c].ap[:, hs].unsqueeze(2).to_broadcast([C, 4, 64]), ALU.mult, [vt, wjT[c]], [xw])
                pa = nxt(PA, "pa")
                MM(pa.ap[0:C, 0:C], kF.ap[:, cs], qF.ap[:, cs], [kF, qF], [pa])
                cb_ = alt(CBs)
                CP("act", cb_.ap, pa.ap[0:C, 0:C], [pa], [cb_])
                pg = nxt(PA, "pa")
                zt = alt(Zt)
                TT("dve", zt.ap.rearrange("p (h i) -> p h i", i=C), GF[c].ap.unsqueeze(1).to_broadcast([16, 4, C]),
                   cf("eye16p", 4 * g, 4 * g + 4)[0:16, :].unsqueeze(2).to_broadcast([16, 4, C]), ALU.mult, [GF[c], cstT], [zt])
                MM(pg.ap[0:C, 0:256], ones_f[0:16, 0:C], zt.ap, [cstT, zt], [pg])
                Lt = alt(Lf)
                L3 = Lt.ap.rearrange("p (h i) -> p h i", i=C)
                TT("dve", L3, pg.ap[0:C, 0:256].rearrange("p (h i) -> p h i", i=C), GT[c].ap[:, hs].unsqueeze(2).to_broadcast([C, 4, C]), ALU.subtract,
                   [pg, GT[c]], [Lt])
                TT("dve", L3, L3, cf("negmask")[0:C, :].unsqueeze(1).to_broadcast([C, 4, C]), ALU.add, [Lt, cstT], [Lt])
                ACT(Lt.ap, Lt.ap, AF.Exp, [Lt], [Lt])
                TT("dve", L3, L3, dtT[c].ap[:, hs].unsqueeze(2).to_broadcast([C, 4, C]), ALU.mult, [Lt, dtT[c]], [Lt])
                am = alt(attm)
                TT("dve", am.ap.rearrange("p (h i) -> p h i", i=C), L3, cb_.ap.unsqueeze(1).to_broadcast([C, 4, C]), ALU.mult, [Lt, cb_], [am])
                po = nxt(PO, "po")
                for hh in range(4):
                    MM(po.ap[0:C, hh * 64:(hh + 1) * 64], am.ap[:, hh * C:(hh + 1) * C], vt.ap[:, hh * 64:(hh + 1) * 64], [am, vt], [po])
                po2 = nxt(PO, "po")
                MM(po2.ap[0:C, 0:256], qF.ap[:, cs], Sb.ap, [qF, Sb], [po2])
                o32 = alt(of32)
                TT("dve", o32.ap.rearrange("p (h d) -> p h d", d=64), po2.ap[0:C, 0:256].rearrange("p (h d) -> p h d", d=64),
                   eGT[c].ap[:, hs].unsqueeze(2).to_broadcast([C, 4, 64]), ALU.mult, [po2, eGT[c]], [o32])
                TT("dve", o32.ap, o32.ap, po.ap[0:C, 0:256], ALU.add, [o32, po], [o32])
                TT("dve", o32.ap, o32.ap, xd.ap, ALU.add, [o32, xd], [o32])
                ps = nxt(PS, "ps")
                MM(ps.ap[:, 0:256], kd.ap, xw.ap, [kd, xw], [ps])
                for hh in range(4):
                    STT(S.ap[:, hh * 64:(hh + 1) * 64], S.ap[:, hh * 64:(hh + 1) * 64], eGtot[c].ap[:, 4 * g + hh:4 * g + hh + 1], ps.ap[:, hh * 64:(hh + 1) * 64],
                        ALU.mult, ALU.add, [S, eGtot[c], ps], [S])
                CP("act", Sb.ap, S.ap, [S], [Sb])
                for j in range(2):
                    pa = nxt(PA, "pa")
                    TR(pa.ap[:, 0:C], o32.ap[:, j * 128:(j + 1) * 128], ident_f[0:C, 0:C], [o32, cstT], [pa])
                    ub = alt(ubuf)
                    TT("dve", ub.ap, pa.ap[:, 0:C], gF[j].ap[:, cs], ALU.mult, [pa, gF[j]], [ub])
                    ACT(SQ[2 * g + j].ap[:, cs], ub.ap, AF.Square, [ub], [SQ[2 * g + j]])
                    TS("dve", YB[2 * g + j].ap[:, cs], ub.ap, vc(l, "ssm_norm", 2 * g + j, 2 * g + j + 1), None, ALU.mult, ALU.bypass, [ub, vecT], [YB[2 * g + j]])
        pd = nxt(PD, "pd")
        for t in range(8):
            MM(pd.ap[:, 0:N], ones_b, SQ[t].ap[:, 0:N], [cbT, SQ[t]], [pd], start=(t == 0), stop=(t == 7))
        RSQ(rstdM.ap[:, 0:N], pd.ap[:, 0:N], 1.0 / D, [pd], [rstdM])

    def merge_branch(l, n, N):
        for _ in merge_gen(l, n, N):
            pass

    def merge_gen(l, n, N):
        dense_mode[0] = True
        yield from _merge_branch(l, n, N)
        dense_mode[0] = False

    def _merge_branch(l, n, N):
        for u in range(4):
            wb_, wbv = LW(w_branch[l, n][:, u * 256:(u + 1) * 256], 8, 256)
            wg_, wgv = LW(w_in[l][:, 10256 + n * 1024 + u * 256:10256 + n * 1024 + (u + 1) * 256], 8, 256)
            for mm in range(2):
                m = u * 2 + mm
                pb = proj(wb_, wbv, mm * 128, YB, N)
                pg = proj(wg_, wgv, mm * 128, XN, N)
                sg = nxt(TMPF, "tmp")
                ACT(sg.ap[:, 0:N], pg.ap[:, 0:N], AF.Sigmoid, [pg], [sg])
                if n == 2:
                    TT("pool", sg.ap[:, 0:N], sg.ap[:, 0:N], rstdM.ap[:, 0:N], ALU.mult, [sg, rstdM], [sg])
                if n == 0:
                    TT("dve", MG[m].ap[:, 0:N], pb.ap[:, 0:N], sg.ap[:, 0:N], ALU.mult, [pb, sg], [MG[m]])
                else:
                    TT("dve", sg.ap[:, 0:N], pb.ap[:, 0:N], sg.ap[:, 0:N], ALU.mult, [pb, sg], [sg])
                    TT("pool", MG[m].ap[:, 0:N], MG[m].ap[:, 0:N], sg.ap[:, 0:N], ALU.add, [MG[m], sg], [MG[m]])
                yield

    def out_proj_and_ffn(l, N):
        dense_mode[0] = True
        _out_proj_and_ffn(l, N)
        dense_mode[0] = False

    def _out_proj_and_ffn(l, N):
        for t in range(8):
            CP("act", YB[t].ap[:, 0:N], MG[t].ap[:, 0:N], [MG[t]], [YB[t]])
        for u in range(4):
            wo_, wov = LW(w_out[l][:, u * 256:(u + 1) * 256], 8, 256)
            for mm in range(2):
                p = proj(wo_, wov, mm * 128, YB, N)
                CP("act", MG[u * 2 + mm].ap[:, 0:N], p.ap[:, 0:N], [p], [MG[u * 2 + mm]])
        resid_add(l, "n_mix_post", N)
        rmsnorm_to_bf16(X, "n_ffn_pre", l, N, XN)
        for half in range(4):
            for u in range(4):
                wu_, wuv = LW(w_up[l][:, half * 1024 + u * 256:half * 1024 + (u + 1) * 256], 8, 256)
                for mm in range(2):
                    p = proj(wu_, wuv, mm * 128, XN, N)
                    tmp = nxt(TMPF, "tmp")
                    ACT(tmp.ap[:, 0:N], p.ap[:, 0:N], AF.Relu, [p], [tmp])
                    TT("pool", HB[u * 2 + mm].ap[:, 0:N], tmp.ap[:, 0:N], tmp.ap[:, 0:N], ALU.mult, [tmp], [HB[u * 2 + mm]])
            for m in range(8):
                wd_, wdv = LW(w_down[l][half * 1024:(half + 1) * 1024, m * 128:(m + 1) * 128], 8, 128)
                p = proj(wd_, wdv, 0, HB, N, KT=8)
                if half == 0:
                    CP("act", MG[m].ap[:, 0:N], p.ap[:, 0:N], [p], [MG[m]])
                else:
                    TT("dve", MG[m].ap[:, 0:N], p.ap[:, 0:N], MG[m].ap[:, 0:N], ALU.add, [p, MG[m]], [MG[m]])
        resid_add(l, "n_ffn_post", N)


    SSB = [tl([128, 1024], F32) for _ in range(2)]
    bs1 = tl([64, 512], F32)
    bs2 = tl([64, 512], F32)
    eGhB = tl([128, TB], F32)
    _sa = SSB[0].ap.bitcast(BF16)
    _sb = SSB[1].ap.bitcast(BF16)
    IF = [BufSet(), BufSet()]
    IF[0].qF, IF[0].kF, IF[0].qdF, IF[0].kdF, IF[0].vF, IF[0].gF, IF[0].eG = qF, kF, qdF, kdF, vF, gF, UE[1]
    IF[1].qF = T(_sa[:, 0:512], SSB[0])
    IF[1].kF = T(_sa[:, 512:1024], SSB[0])
    IF[1].qdF = T(_sa[:, 1024:1536], SSB[0])
    IF[1].kdF = T(_sa[:, 1536:2048], SSB[0])
    IF[1].vF = [T(_sb[:, 0:512], SSB[1]), T(_sb[:, 512:1024], SSB[1])]
    IF[1].gF = [T(_sb[:, 1024:1536], SSB[1]), T(_sb[:, 1536:2048], SSB[1])]
    IF[1].eG = eGhB
    q_all = tl([128, 256], F32)
    kk_all = tl([128, 256], F32)
    a_all = tl([128, 256], F32)
    vtm_all = tl([16, 1024], BF16)
    oacc = SSB[1]
    gS = tl([128, 8, NS], BF16)
    ErepT = tl([16, 16 * 128], BF16)
    hconv = tl([128, 16 * NS * 3], F32)
    oconv = tl([128, 16 * NS * 3], F32)
    s8 = [tl([16, 8], F32) for _ in range(2)]
    onS = SSB[0]
    dtS = tl([16, 16], F32)
    gSd = tl([16, 16], F32)
    Zs = tl([16, 256], F32)
    drep = tl([128, 256], F32)
    arep = a_all
    xsF = [tl([128, NS], BF16) for _ in range(8)]
    bcF = [tl([128, NS], F32) for _ in range(8)]
    P.op("pool", lambda e: e.tensor_copy(out=ErepT.ap.rearrange("p (b m) -> p b m", m=128), in_=cf("eye16p")[0:16, :].unsqueeze(2).to_broadcast([16, 16, 128])),
         [cstT], [ErepT])
    eye3 = cf("eye16rep").rearrange("p (b t) -> p b t", t=16)
    cbE = tl([128, 256], BF16)
    CP("act", cbE.ap, cf("eye16rep"), [cstT], [cbE])
    eye3b = cbE.ap.rearrange("p (b t) -> p b t", t=16)
    tBb = [qdF, kdF]

    def v3h(t, H):
        return t.ap.rearrange("p (h b) -> p h b", b=NS)[:, 0:H, :] if t.ap.shape[1] == 256 else None

    def sample_rec(l, H, dv, sin_dram, sout_dram):
        q3 = q_all.ap[:, 0:H * NS].rearrange("p (h b) -> p h b", b=NS)
        k3 = kk_all.ap[:, 0:H * NS].rearrange("p (h b) -> p h b", b=NS)
        a3 = a_all.ap[:, 0:H * NS].rearrange("p (h b) -> p h b", b=NS)
        hh = H // 2
        for b in range(NS):
            S = alt(SSB)
            S3 = S.ap.rearrange("p (h v) -> p h v", h=H)
            P.dma("sp", S3, sin_dram[l, b].rearrange("h k v -> k h v"), writes=[S], hoist=True)
            pv = [nxt(PD, "pd"), nxt(PD, "pd")]
            for j in range(2):
                MM(pv[j].ap[:, 0:512], ErepT.ap[:, b * 128:(b + 1) * 128], vtm_all.ap[:, j * 512:(j + 1) * 512], [ErepT, vtm_all], [pv[j]])
            tA = (f1, f2)
            tB = (f3, f4)
            for j in range(2):
                TT("dve", tA[j].ap.rearrange("p (h v) -> p h v", h=hh), pv[j].ap[:, 0:512].rearrange("p (h v) -> p h v", h=hh),
                   k3[:, j * hh:(j + 1) * hh, b:b + 1].to_broadcast([128, hh, dv]), ALU.mult, [pv[j], kk_all], [tA[j]])
            if H > 8:
                TT("pool", S3, S3, a3[:, :, b:b + 1].to_broadcast([128, H, dv]), ALU.mult, [S, a_all], [S])
            else:
                for h_ in range(H):
                    ACT(S3[:, h_, :], S3[:, h_, :], AF.Copy, [S, a_all], [S], scale=a3[:, h_, b:b + 1])
            for j in range(2):
                TT("dve", S.ap[:, j * 512:(j + 1) * 512], S.ap[:, j * 512:(j + 1) * 512], tA[j].ap, ALU.add, [S, tA[j]], [S])
            for j in range(2):
                TT("dve", tBb[j].ap.rearrange("p (h v) -> p h v", h=hh), S3[:, j * hh:(j + 1) * hh, :], q3[:, j * hh:(j + 1) * hh, b:b + 1].to_broadcast([128, hh, dv]),
                   ALU.mult, [S, q_all], [tBb[j]])
                MM(psF[0:16, 5 + j, :], eye3b[:, b, :], tBb[j].ap, [cbE, tBb[j]], [PO[j]], start=(b == 0), stop=(b == NS - 1))
            P.dma("sp", sout_dram[l, b].rearrange("h k v -> k h v"), S3, reads=[S])
        for j in range(2):
            CP("dve", oacc.ap[0:NS, j * 512:(j + 1) * 512], psF[0:16, 5 + j, :], [PO[j]], [oacc])

    def to_tm16(src_ap, reads, col0):
        pt = nxt(PT, "pt")
        TR(pt.ap[0:NS, 0:128], src_ap, ident_b, reads + [cbT], [pt])
        CP("act", vtm_all.ap[:, col0:col0 + 128], pt.ap[0:NS, 0:128], [pt], [vtm_all])

    def tm_to_y(l, src_T, t, gain_name, gate_ap, gate_T, dstY, sq=None):
        pa = nxt(PA, "pa")
        TR(pa.ap[:, 0:NS], src_T.ap[0:NS, t * 128:(t + 1) * 128], ident_f[0:NS, 0:NS], [src_T, cstT], [pa])
        if sq is None:
            STT(dstY[t].ap[:, 0:NS], pa.ap[:, 0:NS], vc(l, gain_name, t, t + 1), gate_ap, ALU.mult, ALU.mult, [pa, vecT, gate_T], [dstY[t]])
        else:
            ub = alt(ubuf)
            TT("dve", ub.ap[:, 0:NS], pa.ap[:, 0:NS], gate_ap, ALU.mult, [pa, gate_T], [ub])
            ACT(SQ[t].ap[:, 0:NS], ub.ap[:, 0:NS], AF.Square, [ub], [SQ[t]])
            TS("dve", dstY[t].ap[:, 0:NS], ub.ap[:, 0:NS], vc(l, gain_name, t, t + 1), None, ALU.mult, ALU.bypass, [ub, vecT], [dstY[t]])

    def retention_sample(l):
        N = NS
        q3 = q_all.ap[:, 0:4 * NS].rearrange("p (h b) -> p h b", b=NS)
        k3 = kk_all.ap[:, 0:4 * NS].rearrange("p (h b) -> p h b", b=NS)
        a3 = a_all.ap[:, 0:4 * NS].rearrange("p (h b) -> p h b", b=NS)
        CP("dve", a3, cf("gam1").unsqueeze(2).to_broadcast([128, 4, NS]), [cstT], [a_all])
        for h in range(4):
            for (c0, dst3, scl) in ((h * 128, q3, 1.0), (512 + h * 128, k3, 128 ** -0.5)):
                w1, w1v = LW(w_in[l][:, c0:c0 + 128], 8, 128)
                p1 = proj(w1, w1v, 0, XN, N)
                w2, w2v = LW(None, 8, 128, parts=[(0, 64, w_in[l][:, c0 + 64:c0 + 128]), (64, 64, w_in[l][:, c0:c0 + 64])])
                p2 = proj(w2, w2v, 0, XN, N)
                TT("dve", f1.ap[:, 0:N], p1.ap[:, 0:N], cosT.ap[:, 0:N], ALU.mult, [p1, cosT], [f1])
                TT("dve", f2.ap[:, 0:N], p2.ap[:, 0:N], sinT.ap[:, 0:N], ALU.mult, [p2, sinT], [f2])
                TT("dve", f1.ap[:, 0:N], f1.ap[:, 0:N], f2.ap[:, 0:N], ALU.add, [f1, f2], [f1])
                TS("dve", dst3[:, h, :], f1.ap[:, 0:N], scl, None, ALU.mult, ALU.bypass, [f1], [q_all if dst3 is q3 else kk_all])
            wv_, wvv = LW(w_in[l][:, 1024 + h * 256:1024 + (h + 1) * 256], 8, 256)
            wg_, wgv = LW(w_in[l][:, 2048 + h * 256:2048 + (h + 1) * 256], 8, 256)
            for j in range(2):
                p = proj(wv_, wvv, j * 128, XN, N)
                CP("act", vF[j].ap[:, 0:N], p.ap[:, 0:N], [p], [vF[j]])
                to_tm16(vF[j].ap[:, 0:N], [vF[j]], h * 256 + j * 128)
                p = proj(wg_, wgv, j * 128, XN, N)
                ACT(gS.ap[:, h * 2 + j, :], p.ap[:, 0:N], AF.Silu, [p], [gS])
        sample_rec(l, 4, 256, sret, oret)
        for h in range(4):
            s6 = alt(s8)
            P.op("dve", lambda e, s6=s6, h=h: e.bn_stats(out=s6.ap[:, 0:6], in_=oacc.ap[0:NS, h * 256:(h + 1) * 256]), [oacc], [s6])
            P.op("dve", lambda e, s6=s6: e.bn_aggr(out=s6.ap[:, 6:8], in_=s6.ap[:, 0:6]), [s6], [s6])
            ACT(s6.ap[:, 7:8], s6.ap[:, 7:8], AF.Sqrt, [s6], [s6], bias=EPS)
            P.op("dve", lambda e, s6=s6: e.reciprocal(out=s6.ap[:, 7:8], in_=s6.ap[:, 7:8]), [s6], [s6])
            STT(s6.ap[:, 6:7], s6.ap[:, 6:7], -1.0, s6.ap[:, 7:8], ALU.mult, ALU.mult, [s6], [s6])
            ACT(onS.ap[0:NS, h * 256:(h + 1) * 256], oacc.ap[0:NS, h * 256:(h + 1) * 256], AF.Identity, [oacc, s6], [onS], scale=s6.ap[:, 7:8], bias=s6.ap[:, 6:7])
        for t in range(8):
            tm_to_y(l, onS, t, "ret_norm", gS.ap[:, t, :], gS, YB)

    def hgrn_sample(l):
        N = NS
        q3 = q_all.ap[:, 0:8 * NS].rearrange("p (h b) -> p h b", b=NS)
        k3 = kk_all.ap[:, 0:8 * NS].rearrange("p (h b) -> p h b", b=NS)
        a3 = a_all.ap[:, 0:8 * NS].rearrange("p (h b) -> p h b", b=NS)
        for h in range(8):
            wq, wqv = LW(w_in[l][:, 3072 + h * 128:3072 + (h + 1) * 128], 8, 128)
            p = proj(wq, wqv, 0, XN, N)
            ACT(q3[:, h, :], p.ap[:, 0:N], AF.Silu, [p], [q_all])
            wf, wfv = LW(w_in[l][:, 4096 + h * 128:4096 + (h + 1) * 128], 8, 128)
            p = proj(wf, wfv, 0, XN, N)
            ACT(f2.ap[:, 0:N], p.ap[:, 0:N], AF.Sigmoid, [p], [f2])
            TS("dve", f2.ap[:, 0:N], f2.ap[:, 0:N], omlT.ap[:, l, h:h + 1], lbT.ap[:, l, h:h + 1], ALU.mult, ALU.add, [f2, omlT, lbT], [f2])
            TS("dve", a3[:, h, :], f2.ap[:, 0:N], 1e-6, 1.0, ALU.max, ALU.min, [f2], [a_all])
            TS("dve", k3[:, h, :], a3[:, h, :], -1.0, 1.0, ALU.mult, ALU.add, [a_all], [kk_all])
            wv_, wvv = LW(w_in[l][:, 5120 + h * 128:5120 + (h + 1) * 128], 8, 128)
            p = proj(wv_, wvv, 0, XN, N)
            CP("act", vF[0].ap[:, 0:N], p.ap[:, 0:N], [p], [vF[0]])
            to_tm16(vF[0].ap[:, 0:N], [vF[0]], h * 128)
            wt_, wtv = LW(w_in[l][:, 6144 + h * 128:6144 + (h + 1) * 128], 8, 128)
            p = proj(wt_, wtv, 0, XN, N)
            ACT(gS.ap[:, h, :], p.ap[:, 0:N], AF.Sigmoid, [p], [gS])
        sample_rec(l, 8, 128, shg, ohg)
        for h in range(8):
            s6 = alt(s8)
            P.op("act", lambda e, s6=s6, h=h: e.activation(out=onS.ap[0:NS, h * 128:(h + 1) * 128], in_=oacc.ap[0:NS, h * 128:(h + 1) * 128], func=AF.Square,
                                                         accum_out=s6.ap[:, 0:1]), [oacc], [onS, s6])
            ACT(s6.ap[:, 1:2], s6.ap[:, 0:1], AF.Sqrt, [s6], [s6], scale=1.0 / 128, bias=EPS)
            P.op("dve", lambda e, s6=s6: e.reciprocal(out=s6.ap[:, 1:2], in_=s6.ap[:, 1:2]), [s6], [s6])
            ACT(onS.ap[0:NS, h * 128:(h + 1) * 128], oacc.ap[0:NS, h * 128:(h + 1) * 128], AF.Copy, [oacc, s6], [onS], scale=s6.ap[:, 1:2])
        for t in range(8):
            tm_to_y(l, onS, t, "hg_norm", gS.ap[:, t, :], gS, YB)

    def ssd_sample(l):
        N = NS
        q3 = q_all.ap.rearrange("p (h b) -> p h b", b=NS)
        k3 = kk_all.ap.rearrange("p (h b) -> p h b", b=NS)
        h3 = hconv.ap.rearrange("p (c b w) -> p c b w", b=NS, w=3)
        o3 = oconv.ap.rearrange("p (c b w) -> p c b w", b=NS, w=3)
        P.dma("sp", hconv.ap, sconvP[l], writes=[hconv], hoist=True)
        wd, wdv = LW(w_in[l][:, 10240:10256], 8, 16)
        pa = nxt(PA, "pa")
        for kt in range(8):
            MM(pa.ap[0:NS, 0:16], XN[kt].ap[:, 0:NS], wdv[:, kt, :], [XN[kt], wd], [pa], start=(kt == 0), stop=(kt == 7))
        TT("dve", dtS.ap, pa.ap[0:NS, 0:16], vc(l, "dt_bias", 0, 16)[0:NS, :], ALU.add, [pa, vecT], [dtS])
        ACT(dtS.ap, dtS.ap, AF.Exp, [dtS], [dtS])
        ACT(dtS.ap, dtS.ap, AF.Ln, [dtS], [dtS], bias=1.0)
        TT("dve", gSd.ap, dtS.ap, AnT.ap[0:NS, l, :], ALU.mult, [dtS, AnT], [gSd])
        ACT(gSd.ap, gSd.ap, AF.Exp, [gSd], [gSd])
        for (src, dstrep) in ((dtS, drep), (gSd, arep)):
            TT("dve", Zs.ap.rearrange("p (h b) -> p h b", b=NS), src.ap.unsqueeze(2).to_broadcast([NS, 16, NS]),
               cf("eye16p")[0:NS, :].unsqueeze(1).to_broadcast([NS, 16, NS]), ALU.mult, [src, cstT], [Zs])
            pa = nxt(PA, "pa")
            MM(pa.ap[:, 0:256], ones_f[0:NS, :], Zs.ap, [cstT, Zs], [pa])
            CP("act", dstrep.ap, pa.ap[:, 0:256], [pa], [dstrep])
        for g in range(4):
            tiles = [(2 * g, 8192 + g * 256, 256, 0), (2 * g + 1, None, None, 128), (8 + g, 9216 + g * 128, 128, 0), (12 + g, 9728 + g * 128, 128, 0)]
            for k, (ct, wc0, wn, m0) in enumerate(tiles):
                ue = UE[k % 2]
                if wc0 is not None:
                    wt_, wv_ = LW(w_in[l][:, wc0:wc0 + wn], 8, wn)
                p = proj(wt_, wv_, m0, XN, N)
                CP("act", ue.ap[:, 3:3 + N], p.ap[:, 0:N], [p], [ue])
                CP("dve", o3[:, ct, :, 0:2], h3[:, ct, :, 1:3], [hconv], [oconv])
                CP("dve", o3[:, ct, :, 2], ue.ap[:, 3:3 + N], [ue], [oconv])
                if ct < 8:
                    dst = xsF[ct]
                else:
                    dst = bcF[ct - 8]
                conv_tile(l, ct, ue, N, dst, hist=(h3[:, ct, :, :], hconv))
                if ct < 8:
                    to_tm16(dst.ap[:, 0:N], [dst], ct * 128)
            wz, wzv = LW(w_in[l][:, 7168 + g * 256:7168 + (g + 1) * 256], 8, 256)
            for j in range(2):
                p = proj(wz, wzv, j * 128, XN, N)
                ACT(gS.ap[:, 2 * g + j, :], p.ap[:, 0:N], AF.Silu, [p], [gS])
            Bg = bcF[g]
            Cg = bcF[4 + g]
            CP("dve", q3[:, 4 * g:4 * g + 4, :], Cg.ap.unsqueeze(1).to_broadcast([128, 4, NS]), [Cg], [q_all])
            TT("dve", k3[:, 4 * g:4 * g + 4, :], drep.ap.rearrange("p (h b) -> p h b", b=NS)[:, 4 * g:4 * g + 4, :], Bg.ap.unsqueeze(1).to_broadcast([128, 4, NS]),
               ALU.mult, [drep, Bg], [kk_all])
        P.dma("sp", oconvP[l], oconv.ap, reads=[oconv])
        sample_rec(l, 16, 64, sssm, ossm)
        TT("dve", onS.ap[0:NS, :].rearrange("p (h d) -> p h d", d=64), vtm_all.ap.rearrange("p (h d) -> p h d", d=64),
           vc(l, "d_skip", 0, 16)[0:NS, :].unsqueeze(2).to_broadcast([NS, 16, 64]), ALU.mult, [vtm_all, vecT], [onS])
        TT("dve", onS.ap[0:NS, :], onS.ap[0:NS, :], oacc.ap[0:NS, :], ALU.add, [onS, oacc], [onS])
        for t in range(8):
            tm_to_y(l, onS, t, "ssm_norm", gS.ap[:, t, :], gS, YB, sq=True)
        pd = nxt(PD, "pd")
        for t in range(8):
            MM(pd.ap[:, 0:N], ones_b, SQ[t].ap[:, 0:N], [cbT, SQ[t]], [pd], start=(t == 0), stop=(t == 7))
        RSQ(rstdM.ap[:, 0:N], pd.ap[:, 0:N], 1.0 / D, [pd], [rstdM])

    def sample_block():
        N = NS
        xs3 = xsT.rearrange("(t p) n -> p t n", p=128)
        ys3 = ysT.rearrange("(t p) n -> p t n", p=128)
        for t in range(8):
            P.dma("sp", X[t].ap[:, 0:N], xs3[:, t, :], writes=[X[t]])
        P.dma("sp", cosT.ap[:, 0:N], rope[:, SEQ:SEQ + NS], writes=[cosT])
        P.dma("sp", sinT.ap[:, 0:N], rope[:, SEQ + NS + SEQ:SEQ + NS + SEQ + NS], writes=[sinT])
        for l in range(DBG["layers"]):
            mark("S l%d norm" % l)
            rmsnorm_to_bf16(X, "n_mix_pre", l, N, XN)
            if "ret" in DBG["stages"]:
                mark("S l%d ret" % l)
                retention_sample(l)
                merge_branch(l, 0, N)
            if "hg" in DBG["stages"]:
                mark("S l%d hg" % l)
                hgrn_sample(l)
                merge_branch(l, 1, N)
            if "ssd" in DBG["stages"]:
                mark("S l%d ssd" % l)
                ssd_sample(l)
                merge_branch(l, 2, N)
            if "ffn" in DBG["stages"]:
                mark("S l%d outffn" % l)
                out_proj_and_ffn(l, N)
        for t in range(8):
            P.dma("sp", ys3[:, t, :], X[t].ap[:, 0:N], reads=[X[t]])

    xT3 = xT.rearrange("(t p) n -> p t n", p=128)
    yT3 = yT.rearrange("(t p) n -> p t n", p=128)
    for blk in range(DBG["blocks"]):
        N = TB
        c0 = blk * TB
        for t in range(8):
            P.dma("sp", X[t].ap, xT3[:, t, c0:c0 + TB], writes=[X[t]], hoist=True)
        P.dma("sp", cosT.ap, rope[:, c0:c0 + TB], writes=[cosT], hoist=True)
        P.dma("sp", sinT.ap, rope[:, SEQ + NS + c0:SEQ + NS + c0 + TB], writes=[sinT], hoist=True)
        for l in range(DBG["layers"]):
            mark("b%d l%d norm" % (blk, l))
            rmsnorm_to_bf16(X, "n_mix_pre", l, N, XN)
            full = all(x in DBG["stages"] for x in ("ret", "hg", "ssd")) and not DBG.get("nomergepipe")
            if "ret" in DBG["stages"]:
                mark("b%d l%d ret" % (blk, l))
                retention_block2(l, N)
                mark("b%d l%d merge0" % (blk, l))
                if full:
                    interleave(merge_gen(l, 0, N), hg_front(IF[0], l, N, 0))
                else:
                    merge_branch(l, 0, N)
            if "hg" in DBG["stages"]:
                mark("b%d l%d hg" % (blk, l))
                hgrn_block2(l, N, first_done=full)
                mark("b%d l%d merge1" % (blk, l))
                if full:
                    ssd_dt_prep2(l, N)
                    interleave(merge_gen(l, 1, N), ssd_front(IF[0], l, N, 0))
                else:
                    merge_branch(l, 1, N)
            if "ssd" in DBG["stages"]:
                mark("b%d l%d ssd" % (blk, l))
                ssd_block2(l, N, first_done=full)
                mark("b%d l%d merge2" % (blk, l))
                merge_branch(l, 2, N)
            if "ffn" in DBG["stages"]:
                mark("b%d l%d outffn" % (blk, l))
                out_proj_and_ffn(l, N)
        for t in range(8):
            P.dma("sp", yT3[:, t, c0:c0 + TB], X[t].ap, reads=[X[t]])
    for l in range(DEPTH):
        for h in range(4):
            P.dma("sp", pret[l, h], Sret[l][h].ap, reads=[Sret[l][h]])
        for h in range(8):
            P.dma("sp", phg[l, h], Shg[l][h].ap, reads=[Shg[l][h]])
        for g in range(4):
            for hh in range(4):
                P.dma("sp", pssm[l, 4 * g + hh], Sssm[l][g].ap[:, hh * 64:(hh + 1) * 64], reads=[Sssm[l][g]])
        for ct in range(16):
            P.dma("sp", pconvP[l, :, ct * 3:(ct + 1) * 3], HALO[l][ct].ap, reads=[HALO[l][ct]])

    if DBG.get("samples", True):
        sample_block()
    mark("end")
    P.emit()
    st.close()
    nc._marks = MARKS
    nc._nops = {e: len(v) for e, v in P.ops.items()}
    return nc


_CACHE = {}


def kernel(**inp):
    f = np.float32
    cst, coffs, rope = _consts()
    NCF = cst.shape[1]
    if "nc" not in _CACHE:
        _CACHE["nc"] = build(coffs, NCF)
    nc = _CACHE["nc"]
    vec = np.zeros((128, DEPTH * NVL), f)
    for l in range(DEPTH):
        def put(name, arr):
            o = l * NVL + VOFF[name]
            vec[:, o:o + arr.shape[1]] = arr
        put("n_mix_pre", _fm(inp["norm_mix_pre"][l]))
        put("n_mix_post", _fm(inp["norm_mix_post"][l]))
        put("n_ffn_pre", _fm(inp["norm_ffn_pre"][l]))
        put("n_ffn_post", _fm(inp["norm_ffn_post"][l]))
        put("ret_norm", _fm(inp["ret_norm"][l]))
        put("hg_norm", _fm(inp["hg_norm"][l]))
        put("ssm_norm", _fm(inp["ssm_norm"][l]))
        put("lb0", _fm(inp["hg_lb_logits"][0]))
        put("lb1", _fm(inp["hg_lb_logits"][1]))
        put("conv_w", np.concatenate([_fm(inp["conv_w"][l, w]) for w in range(4)], 1))
        put("conv_b", _fm(inp["conv_b"][l]))
        put("dt_bias", np.broadcast_to(inp["dt_bias"][l][None, :], (128, 16)))
        put("a_log", np.broadcast_to(inp["a_log"][l][None, :], (128, 16)))
        put("d_skip", np.broadcast_to(inp["d_skip"][l][None, :], (128, 16)))
    in_maps = []
    for c in range(8):
        b0 = c * NS
        sc = inp["state_conv"][:, b0:b0 + NS]
        scP = np.ascontiguousarray(sc.reshape(DEPTH, NS, 3, 16, 128).transpose(0, 4, 3, 1, 2)).reshape(DEPTH, 128, 16 * NS * 3)
        in_maps.append({
            "xT": np.ascontiguousarray(inp["x_prompt"][c].T),
            "xsT": np.ascontiguousarray(inp["x_sample"][b0:b0 + NS, 0, :].T),
            "w_in": inp["w_in"], "w_branch": inp["w_branch"], "w_out": inp["w_out"], "w_up": inp["w_up"], "w_down": inp["w_down"],
            "vecs": vec, "cst": cst, "rope": rope,
            "sret": np.ascontiguousarray(inp["state_ret"][:, b0:b0 + NS]),
            "shg": np.ascontiguousarray(inp["state_hgrn"][:, b0:b0 + NS]),
            "sssm": np.ascontiguousarray(inp["state_ssm"][:, b0:b0 + NS]),
            "sconvP": scP,
        })
    res = run_bass_kernel_spmd(nc, in_maps, core_ids=list(range(8)))
    R = res.results
    y_p = np.stack([R[c]["yT"].T for c in range(8)], 0)
    y_s = np.concatenate([R[c]["ysT"].T for c in range(8)], 0)[:, None, :]
    p_ret = np.stack([R[c]["pret"] for c in range(8)], 1)
    p_hg = np.stack([R[c]["phg"] for c in range(8)], 1)
    p_ssm = np.stack([R[c]["pssm"] for c in range(8)], 1)
    p_conv = np.stack([R[c]["pconvP"].reshape(DEPTH, 128, 16, 3).transpose(0, 3, 2, 1).reshape(DEPTH, 3, 2048) for c in range(8)], 1)
    s_ret = np.concatenate([R[c]["oret"] for c in range(8)], 1)
    s_hg = np.concatenate([R[c]["ohg"] for c in range(8)], 1)
    s_ssm = np.concatenate([R[c]["ossm"] for c in range(8)], 1)
    s_conv = np.concatenate([R[c]["oconvP"].reshape(DEPTH, 128, 16, NS, 3).transpose(0, 3, 4, 2, 1).reshape(DEPTH, NS, 3, 2048) for c in range(8)], 1)
    return (np.ascontiguousarray(y_p, f), np.ascontiguousarray(y_s, f), np.ascontiguousarray(p_ret, f), np.ascontiguousarray(p_hg, f),
            np.ascontiguousarray(p_ssm, f), np.ascontiguousarray(p_conv, f), np.ascontiguousarray(s_ret, f), np.ascontiguousarray(s_hg, f),
            np.ascontiguousarray(s_ssm, f), np.ascontiguousarray(s_conv, f))
```
